# Optimizing a Trainium2 kernel written in Bass

```python
import math
import jax
import jax.numpy as jnp
from jax import lax
import numpy as np

D_MODEL = 1024
BATCH = 8
SEQ = 4096
DEPTH = 2

HEAD_DIM = 64
ROT_DIM = HEAD_DIM // 4
ROPE_THETA = 500000.0
NORM_EPS = 1e-6
NEG_INF = -1e30
ATTN_SCALE = HEAD_DIM ** -0.5
Q_CHUNK = 32

N_BRANCH = 3
BRANCH_HEADS = 8
BRANCH_WIDTH = BRANCH_HEADS * HEAD_DIM

DIL_PATTERNS = ((128, 1), (512, 4), (2048, 16))
N_GROUPS_A = len(DIL_PATTERNS)

IDX_HEADS = 8
IDX_DIM = 64
IDX_ROT = IDX_DIM // 4
IDX_TOPK_MAX = 256

MOBA_BLOCK = 256
MOBA_TOPK = 3

A_QKV_COLS = 3 * N_GROUPS_A * BRANCH_WIDTH
B_QKV_COLS = 3 * BRANCH_WIDTH
IDX_Q_COLS = IDX_HEADS * IDX_DIM
IDX_K_COLS = IDX_DIM
IDX_W_COLS = IDX_HEADS
C_QKV_COLS = 3 * BRANCH_WIDTH
SILU_GATE_COLS = N_BRANCH * BRANCH_WIDTH
MERGE_GATE_COLS = N_BRANCH * D_MODEL
IN_COLS = (A_QKV_COLS, B_QKV_COLS, IDX_Q_COLS, IDX_K_COLS, IDX_W_COLS,
           C_QKV_COLS, SILU_GATE_COLS, MERGE_GATE_COLS)
N_IN = sum(IN_COLS)
SPLIT_POINTS = tuple(int(c) for c in np.cumsum(IN_COLS)[:-1])

kernel_name = 'hybrid_dilated_dsa_moba_block'


def rms_norm(x, g):
    xf = x.astype(jnp.float32)
    y = xf * lax.rsqrt(jnp.mean(xf * xf, axis=-1, keepdims=True) + NORM_EPS)
    return (y * g.astype(jnp.float32)).astype(x.dtype)


def rope_tables(positions, rot_dim):
    inv = ROPE_THETA ** (-jnp.arange(0, rot_dim, 2, dtype=jnp.float32) / rot_dim)
    ang = positions.astype(jnp.float32)[..., None] * inv
    return jnp.cos(ang)[:, :, None, :], jnp.sin(ang)[:, :, None, :]


def partial_rope(x, cos, sin):
    half = cos.shape[-1]
    xf = x.astype(jnp.float32)
    x1, x2, rest = xf[..., :half], xf[..., half:2 * half], xf[..., 2 * half:]
    out = jnp.concatenate([x1 * cos - x2 * sin, x2 * cos + x1 * sin, rest], axis=-1)
    return out.astype(x.dtype)


def banded_attention(q, k, v, band):
    N, n, dh = q.shape
    nb = -(-n // band)
    pad = nb * band - n
    qb = jnp.pad(q, ((0, 0), (0, pad), (0, 0))).reshape(N, nb, band, dh)
    kp = jnp.pad(k, ((0, 0), (band, pad), (0, 0)))
    vp = jnp.pad(v, ((0, 0), (band, pad), (0, 0)))

    def windows(t):
        return jnp.concatenate([t[:, :-band].reshape(N, nb, band, dh),
                                t[:, band:].reshape(N, nb, band, dh)], axis=2)

    kw, vw = windows(kp), windows(vp)
    s = jnp.einsum('nbqd,nbkd->nbqk', qb, kw, preferred_element_type=jnp.float32) * ATTN_SCALE
    i = jnp.arange(band)[:, None]
    j = jnp.arange(2 * band)[None, :]
    dist = i + band - j
    blk = jnp.arange(nb)[:, None, None]
    valid = (dist >= 0) & (dist <= band) & ((blk > 0) | (j >= band))
    s = jnp.where(valid, s, NEG_INF)
    m = jnp.max(s, axis=-1, keepdims=True)
    p = jnp.exp(s - m)
    den = jnp.sum(p, axis=-1, keepdims=True)
    o = jnp.einsum('nbqk,nbkd->nbqd', (p / den).astype(v.dtype), vw)
    lse = (m + jnp.log(den))[..., 0]
    return o.reshape(N, nb * band, dh)[:, :n], lse.reshape(N, nb * band)[:, :n]


def dilated_attention(q, k, v):
    B, S, _, dh = q.shape
    H = BRANCH_HEADS
    q = q.reshape(B, S, N_GROUPS_A, H, dh)
    k = k.reshape(B, S, N_GROUPS_A, H, dh)
    v = v.reshape(B, S, N_GROUPS_A, H, dh)
    outs, lses = [], []
    for g, (window, dil) in enumerate(DIL_PATTERNS):
        n = S // dil

        def fold(t):
            return t[:, :, g].reshape(B, n, dil, H, dh).transpose(0, 2, 3, 1, 4).reshape(B * dil * H, n, dh)

        o, lse = banded_attention(fold(q), fold(k), fold(v), window // dil)
        outs.append(o.reshape(B, dil, H, n, dh).transpose(0, 3, 1, 2, 4).reshape(B, S, H, dh))
        lses.append(lse.reshape(B, dil, H, n).transpose(0, 3, 1, 2).reshape(B, S, H))
    o = jnp.stack(outs, axis=2)
    wts = jax.nn.softmax(jnp.stack(lses, axis=2), axis=2)
    return jnp.einsum('bsgh,bsghd->bshd', wts.astype(o.dtype), o)


def dsa_attention(q, k, v, q_idx, k_idx, w_idx):
    B, S, H, dh = q.shape
    topk = min(IDX_TOPK_MAX, S // 4)
    b_i = jnp.arange(B)[:, None, None]
    s_pos = jnp.arange(S)
    k_idx32 = k_idx.astype(jnp.float32)

    def chunk(c):
        t0 = c * Q_CHUNK
        qc = lax.dynamic_slice_in_dim(q, t0, Q_CHUNK, axis=1)
        qi = lax.dynamic_slice_in_dim(q_idx, t0, Q_CHUNK, axis=1).astype(jnp.float32)
        wi = lax.dynamic_slice_in_dim(w_idx, t0, Q_CHUNK, axis=1).astype(jnp.float32)
        t = t0 + jnp.arange(Q_CHUNK)
        logits = jnp.einsum('bqhd,bsd->bqhs', qi, k_idx32) * (IDX_DIM ** -0.5)
        score = jnp.einsum('bqh,bqhs->bqs', wi * (IDX_HEADS ** -0.5), jax.nn.relu(logits))
        score = jnp.where(s_pos[None, None, :] <= t[None, :, None], score, NEG_INF)
        _, idx = lax.top_k(score, topk)
        ok = idx <= t[None, :, None]
        ks = k[b_i, idx]
        vs = v[b_i, idx]
        s = jnp.einsum('bqhd,bqkhd->bqhk', qc, ks, preferred_element_type=jnp.float32) * ATTN_SCALE
        s = jnp.where(ok[:, :, None, :], s, NEG_INF)
        p = jax.nn.softmax(s, axis=-1)
        return jnp.einsum('bqhk,bqkhd->bqhd', p.astype(v.dtype), vs)

    out = lax.map(chunk, jnp.arange(S // Q_CHUNK))
    return out.transpose(1, 0, 2, 3, 4).reshape(B, S, H, dh)


def moba_attention(q, k, v):
    B, S, H, dh = q.shape
    nblk = -(-S // MOBA_BLOCK)
    pad = nblk * MOBA_BLOCK - S
    kp = jnp.pad(k, ((0, 0), (0, pad), (0, 0), (0, 0)))
    vp = jnp.pad(v, ((0, 0), (0, pad), (0, 0), (0, 0)))
    kb = kp.reshape(B, nblk, MOBA_BLOCK, H, dh).transpose(0, 3, 1, 2, 4)
    vb = vp.reshape(B, nblk, MOBA_BLOCK, H, dh).transpose(0, 3, 1, 2, 4)
    kmean = jnp.mean(kb.astype(jnp.float32), axis=3)
    topb = min(MOBA_TOPK, nblk - 1)
    b_i = jnp.arange(B)[:, None, None, None]
    h_i = jnp.arange(H)[None, :, None, None]
    blk_ids = jnp.arange(nblk)
    in_blk = jnp.arange(MOBA_BLOCK)

    def chunk(c):
        t0 = c * Q_CHUNK
        qc = lax.dynamic_slice_in_dim(q, t0, Q_CHUNK, axis=1).transpose(0, 2, 1, 3)
        t = t0 + jnp.arange(Q_CHUNK)
        own = t0 // MOBA_BLOCK
        k_own = lax.dynamic_index_in_dim(kb, own, axis=2, keepdims=False)
        v_own = lax.dynamic_index_in_dim(vb, own, axis=2, keepdims=False)
        s_own = jnp.einsum('bhqd,bhkd->bhqk', qc, k_own, preferred_element_type=jnp.float32) * ATTN_SCALE
        causal = (own * MOBA_BLOCK + in_blk)[None, :] <= t[:, None]
        s_own = jnp.where(causal, s_own, NEG_INF)
        if topb > 0:
            gate = jnp.einsum('bhqd,bhnd->bhqn', qc.astype(jnp.float32), kmean)
            gate = jnp.where(blk_ids < own, gate, NEG_INF)
            _, bidx = lax.top_k(gate, topb)
            okb = bidx < own
            ks = kb[b_i, h_i, bidx]
            vs = vb[b_i, h_i, bidx]
            s_sel = jnp.einsum('bhqd,bhqnkd->bhqnk', qc, ks, preferred_element_type=jnp.float32) * ATTN_SCALE
            s_sel = jnp.where(okb[..., None], s_sel, NEG_INF).reshape(B, H, Q_CHUNK, topb * MOBA_BLOCK)
            p = jax.nn.softmax(jnp.concatenate([s_sel, s_own], axis=-1), axis=-1).astype(v.dtype)
            p_sel = p[..., :topb * MOBA_BLOCK].reshape(B, H, Q_CHUNK, topb, MOBA_BLOCK)
            p_own = p[..., topb * MOBA_BLOCK:]
            o = (jnp.einsum('bhqnk,bhqnkd->bhqd', p_sel, vs)
                 + jnp.einsum('bhqk,bhkd->bhqd', p_own, v_own))
        else:
            p_own = jax.nn.softmax(s_own, axis=-1).astype(v.dtype)
            o = jnp.einsum('bhqk,bhkd->bhqd', p_own, v_own)
        return o.transpose(0, 2, 1, 3)

    out = lax.map(chunk, jnp.arange(S // Q_CHUNK))
    return out.transpose(1, 0, 2, 3, 4).reshape(B, S, H, dh)


def hybrid_layer(x, cos, sin, cos_i, sin_i, norm_g, w_in, qk_g, w_br, w_out):
    B, S, _ = x.shape
    h = rms_norm(x, norm_g)
    proj = jnp.einsum('bsd,dc->bsc', h, w_in)
    a_qkv, b_qkv, q_idx, k_idx, w_idx, c_qkv, z, g = jnp.split(proj, SPLIT_POINTS, axis=-1)

    def heads(t, n_heads):
        t = t.reshape(B, S, 3, n_heads, HEAD_DIM)
        return t[:, :, 0], t[:, :, 1], t[:, :, 2]

    def prep_qk(q, k, mixer):
        q = partial_rope(rms_norm(q, qk_g[mixer, 0]), cos, sin)
        k = partial_rope(rms_norm(k, qk_g[mixer, 1]), cos, sin)
        return q, k

    qa, ka, va = heads(a_qkv, N_GROUPS_A * BRANCH_HEADS)
    qa, ka = prep_qk(qa, ka, 0)
    o_a = dilated_attention(qa, ka, va)

    qb, kb, vb = heads(b_qkv, BRANCH_HEADS)
    qb, kb = prep_qk(qb, kb, 1)
    q_idx = partial_rope(q_idx.reshape(B, S, IDX_HEADS, IDX_DIM), cos_i, sin_i)
    k_idx = partial_rope(k_idx[:, :, None, :], cos_i, sin_i)[:, :, 0]
    o_b = dsa_attention(qb, kb, vb, q_idx, k_idx, w_idx)

    qc, kc, vc = heads(c_qkv, BRANCH_HEADS)
    qc, kc = prep_qk(qc, kc, 2)
    o_c = moba_attention(qc, kc, vc)

    o = jnp.stack([o_a, o_b, o_c], axis=2).reshape(B, S, N_BRANCH, BRANCH_WIDTH)
    o = o * jax.nn.silu(z.reshape(B, S, N_BRANCH, BRANCH_WIDTH))
    y = jnp.einsum('bsnc,ncd->bsnd', o, w_br)
    gate = jax.nn.sigmoid(g.reshape(B, S, N_BRANCH, D_MODEL))
    merged = jnp.sum(gate * y, axis=2)
    return x + jnp.einsum('bsd,de->bse', merged, w_out)


def setup_inputs(seed: int = 0) -> dict:
    key = jax.random.key(seed)
    ks = jax.random.split(key, 7)
    x = jax.random.normal(ks[0], (BATCH, SEQ, D_MODEL), jnp.float32)
    offsets = jax.random.randint(ks[1], (BATCH, 1), 0, 2048, dtype=jnp.int32)
    positions = (offsets + jnp.arange(SEQ, dtype=jnp.int32)[None, :]).astype(jnp.int32)
    norm_g = 1.0 + 0.1 * jax.random.normal(ks[2], (DEPTH, D_MODEL), jnp.float32)
    w_in = jax.random.normal(ks[3], (DEPTH, D_MODEL, N_IN), jnp.float32) * (D_MODEL ** -0.5)
    qk_g = 1.0 + 0.1 * jax.random.normal(ks[4], (DEPTH, N_BRANCH, 2, HEAD_DIM), jnp.float32)
    w_br = jax.random.normal(ks[5], (DEPTH, N_BRANCH, BRANCH_WIDTH, D_MODEL), jnp.float32) * (BRANCH_WIDTH ** -0.5)
    w_out = jax.random.normal(ks[6], (DEPTH, D_MODEL, D_MODEL), jnp.float32) * (D_MODEL ** -0.5)
    return {'x': x, 'positions': positions, 'norm_g': norm_g, 'w_in': w_in,
            'qk_g': qk_g, 'w_br': w_br, 'w_out': w_out}


def reference(x, positions, norm_g, w_in, qk_g, w_br, w_out):
    cos, sin = rope_tables(positions, ROT_DIM)
    cos_i, sin_i = rope_tables(positions, IDX_ROT)
    for layer in range(DEPTH):
        x = hybrid_layer(x, cos, sin, cos_i, sin_i, norm_g[layer], w_in[layer],
                         qk_g[layer], w_br[layer], w_out[layer])
    return x
```

```python
import math
from contextlib import ExitStack

import numpy as np
import concourse.bass as bass
import concourse.mybir as mybir
from concourse.bass_utils import run_bass_kernel_spmd

F32 = mybir.dt.float32
BF16 = mybir.dt.bfloat16
I32 = mybir.dt.int32
ALU = mybir.AluOpType
AF = mybir.ActivationFunctionType
AX = mybir.AxisListType

S = 4096
NT = 32
D = 1024
NIN = 12872
HD = 64
DILS = (1, 4, 16)
A_OFF = 0
B_OFF = 4608
IQ_OFF = 6144
IK_OFF = 6656
IW_OFF = 6720
C_OFF = 6728
Z_OFF = 8264
G_OFF = 9800
EPS = 1e-6
BIG = 30000.0
TOPK = 256
NBIS = 16
ATT_SCALE = 0.125

ENGS = ("pe", "act", "dve", "pool", "sp")


class Buf:
    __slots__ = ("name", "lw", "rd", "dsem")

    def __init__(self, name):
        self.name = name
        self.lw = None
        self.rd = []
        self.dsem = None


class Tok:
    __slots__ = ("key", "count")

    def __init__(self, key, count=None):
        self.key = key
        self.count = count


class Prog:
    def __init__(self, nc, stack):
        self.nc = nc
        self.stack = stack
        self.ops = {e: [] for e in ENGS}
        self.sems = {}
        self.cnt = {}
        self.known = {e: {} for e in ENGS}
        self.pending = {e: [] for e in ENGS}
        for e in ENGS:
            if e != "sp":
                self._newsem(e)
        self.nbuf = 0
        self.ninstr = 0

    def _newsem(self, key):
        self.sems[key] = self.stack.enter_context(self.nc.semaphore("s%d" % len(self.sems)))
        self.cnt[key] = 0

    def buf(self, name=None):
        self.nbuf += 1
        return Buf("%s_%d" % (name or "b", self.nbuf))

    def bufs(self, n, name="b"):
        return [self.buf(name) for _ in range(n)]

    def _deps(self, reads, writes):
        deps = []
        for b in reads:
            if b.lw is not None:
                deps.append(b.lw)
        for b in writes:
            if b.lw is not None:
                deps.append(b.lw)
            deps.extend(b.rd)
        return deps

    def _emit_waits(self, eng, deps):
        need = {}
        for t in deps:
            if t.key == eng == "pe":
                continue
            if t.count is None:
                raise RuntimeError("dependency on instruction without inc (%s -> %s)" % (t.key, eng))
            if need.get(t.key, 0) < t.count:
                need[t.key] = t.count
        kn = self.known[eng]
        for key, c in need.items():
            if kn.get(key, 0) >= c:
                continue
            kn[key] = c
            sem = self.sems[key]
            self.ops[eng].append(lambda e, sem=sem, c=c: e.wait_ge(sem, c))
            self.ninstr += 1

    def _register(self, tok, reads, writes):
        for b in reads:
            b.rd.append(tok)
        for b in writes:
            b.lw = tok
            b.rd = []

    def op(self, eng, fn, reads=(), writes=(), inc=True):
        self._emit_waits(eng, self._deps(reads, writes))
        tok = Tok(eng)
        self.ninstr += 1
        if inc:
            self.cnt[eng] += 1
            c = self.cnt[eng]
            tok.count = c
            for t in self.pending[eng]:
                t.count = c
            self.pending[eng] = []
            sem = self.sems[eng]
            self.ops[eng].append(lambda e, fn=fn, sem=sem: fn(e).then_inc(sem, 1))
        else:
            self.pending[eng].append(tok)
            self.ops[eng].append(lambda e, fn=fn: fn(e))
        self._register(tok, reads, writes)
        return tok

    def dma(self, out_ap, in_ap, sb, reads=(), writes=(), eng="sp"):
        if sb.dsem is None:
            sb.dsem = "d%d" % len(self.sems)
            self._newsem(sb.dsem)
        key = sb.dsem
        self._emit_waits(eng, self._deps(reads, writes))
        self.cnt[key] += 16
        tok = Tok(key, self.cnt[key])
        sem = self.sems[key]
        self.ninstr += 1
        self.ops[eng].append(
            lambda e, o=out_ap, i=in_ap, sem=sem: e.dma_start(out=o, in_=i).then_inc(sem, 16))
        self._register(tok, reads, writes)
        return tok

    def barrier(self):
        snap = dict(self.cnt)
        for e in ENGS:
            if self.pending[e]:
                raise RuntimeError("barrier with pending instrs on " + e)
        for e in ENGS:
            kn = self.known[e]
            for key, c in snap.items():
                if c > 0 and kn.get(key, 0) < c:
                    kn[key] = c
                    sem = self.sems[key]
                    self.ops[e].append(lambda eng, sem=sem, c=c: eng.wait_ge(sem, c))

    def finish(self, eng="sp"):
        kn = self.known[eng]
        for key, c in self.cnt.items():
            if c > 0 and kn.get(key, 0) < c:
                kn[key] = c
                sem = self.sems[key]
                self.ops[eng].append(lambda e, sem=sem, c=c: e.wait_ge(sem, c))

    def emit(self):
        with self.nc.Block() as block:
            @block.tensor
            def _(e):
                for f in self.ops["pe"]:
                    f(e)

            @block.scalar
            def _(e):
                for f in self.ops["act"]:
                    f(e)

            @block.vector
            def _(e):
                for f in self.ops["dve"]:
                    f(e)

            @block.gpsimd
            def _(e):
                for f in self.ops["pool"]:
                    f(e)

            @block.sync
            def _(e):
                for f in self.ops["sp"]:
                    f(e)


class Rot:
    def __init__(self, items):
        self.items = items
        self.i = 0

    def next(self):
        it = self.items[self.i % len(self.items)]
        self.i += 1
        return it


def tok_slice(g, tile):
    d = DILS[g]
    nb = NT // d
    r, i = tile // nb, tile % nb
    start = 128 * i * d + r
    return slice(start, start + 127 * d + 1, d)


ALL_PHASES = ("p0", "a", "idx", "b", "c", "z", "f")


def build_program(L, dbg=False, phases=ALL_PHASES):
    nc = bass.Bass("TRN2", target_bir_lowering=False)
    okind = "ExternalOutput" if dbg else "Internal"
    x_in = nc.dram_tensor("x", [S, D], F32, kind="ExternalInput").ap()
    posp = nc.dram_tensor("posp", [128, 3 * NT], I32, kind="ExternalInput").ap()
    normg_d = nc.dram_tensor("norm_g", [L, D], F32, kind="ExternalInput").ap()
    win_d = nc.dram_tensor("w_in", [L, D, NIN], F32, kind="ExternalInput").ap()
    qkg_d = nc.dram_tensor("qk_g", [L, 384], F32, kind="ExternalInput").ap()
    wbr_d = nc.dram_tensor("w_br", [L, 1536, D], F32, kind="ExternalInput").ap()
    wout_d = nc.dram_tensor("w_out", [L, D, D], F32, kind="ExternalInput").ap()
    out_d = nc.dram_tensor("out", [S, D], F32, kind="ExternalOutput").ap()
    oa_d = nc.dram_tensor("oa_aug", [S, 3, 8, 65], F32, kind=okind).ap()
    obc_d = nc.dram_tensor("o_bc", [NT, 128, 1024], BF16, kind=okind).ap()
    zs_d = nc.dram_tensor("zs", [NT, 128, 1536], BF16, kind=okind).ap()
    gs_d = nc.dram_tensor("gs", [NT, 128, 3072], BF16, kind=okind).ap()
    mk_d = nc.dram_tensor("maskT", [8, NT, 128, 512], BF16, kind=okind).ap()
    x1_d = nc.dram_tensor("x1s", [S, D], F32, kind="Internal").ap() if L > 1 else None

    with ExitStack() as st:
        P = Prog(nc, st)

        def sb(name, shape, dt):
            return st.enter_context(nc.sbuf_tensor(name, shape, dt))

        def ps(name, shape, dt):
            return st.enter_context(nc.psum_tensor(name, shape, dt))

        PSB = [(ps("psb%d" % i, [128, 512], F32), P.buf("psb")) for i in range(5)]
        OBK = [(ps("obk%d" % i, [128, 512], F32), P.buf("obk")) for i in range(2)]
        TPB = ps("tpb", [128, 8, 128], BF16)
        TBK = P.buf("tpb")
        TB = [TBK]
        pj_rot = Rot([(t[:], b) for t, b in PSB[0:2]])
        st_rot = Rot([(t[:], b) for t, b in PSB[2:5]])
        tp_rot = Rot([(TPB[:, 2 * i:2 * i + 2, :], TBK) for i in range(4)])
        o_rot = Rot([(t[:, 0:128], b) for t, b in OBK])
        o4_rot = Rot([(t[:].rearrange("p (u d) -> p u d", u=4), b) for t, b in (OBK + PSB[0:2])])

        ident = sb("ident", [128, 128], BF16)
        identB = sb("identB", [128, 128], BF16)
        band = sb("band", [128, 256], BF16)
        triq = sb("triq", [128, 128], F32)
        eb = sb("eb", [16, 16, 128], BF16)
        neghalf = sb("neghalf", [128, 8], F32)
        inv2pi = sb("inv2pi", [128, 8], F32)
        pow2 = sb("pow2", [128, NBIS + 1], F32)
        thrneg = sb("thrneg", [128, 1], F32)
        negbig16 = sb("negbig16", [128, 16], F32)
        cosT = sb("cosT", [128, 3 * NT, 8], F32)
        sinT = sb("sinT", [128, 3 * NT, 8], F32)
        CONST = P.buf("const")

        def pool_c(fn):
            P.op("pool", fn, reads=[CONST], writes=[CONST])

        pool_c(lambda e: e.memset(ident[:], 1.0))
        pool_c(lambda e: e.affine_select(out=ident[:], in_=ident[:], pattern=[[-1, 128]], compare_op=ALU.is_equal,
                                         fill=0.0, base=0, channel_multiplier=1))
        pool_c(lambda e: e.memset(identB[:], BIG))
        pool_c(lambda e: e.affine_select(out=identB[:], in_=identB[:], pattern=[[-1, 128]], compare_op=ALU.is_equal,
                                         fill=0.0, base=0, channel_multiplier=1))
        pool_c(lambda e: e.memset(band[:], 0.0))
        pool_c(lambda e: e.affine_select(out=band[:, 0:128], in_=band[:, 0:128], pattern=[[1, 128]], compare_op=ALU.is_ge,
                                         fill=-1.0, base=0, channel_multiplier=-1))
        pool_c(lambda e: e.affine_select(out=band[:, 128:256], in_=band[:, 128:256], pattern=[[-1, 128]], compare_op=ALU.is_ge,
                                         fill=-1.0, base=0, channel_multiplier=1))
        pool_c(lambda e: e.memset(triq[:], 0.0))
        pool_c(lambda e: e.affine_select(out=triq[:], in_=triq[:], pattern=[[-1, 128]], compare_op=ALU.is_ge,
                                         fill=-1e30, base=0, channel_multiplier=1))
        pool_c(lambda e: e.memset(eb[:], BIG))
        pool_c(lambda e: e.affine_select(out=eb[:], in_=eb[:], pattern=[[-1, 16], [0, 128]], compare_op=ALU.is_equal,
                                         fill=0.0, base=0, channel_multiplier=1))
        pool_c(lambda e: e.memset(dstrip[:], 0.0))
        pool_c(lambda e: e.memset(dstrip[:, 0:384], -1.0))
        pool_c(lambda e: e.tensor_copy(out=dstrip[:, 384:512], in_=band[:, 0:128]))
        pool_c(lambda e: e.memset(neghalf[:], -0.5))
        pool_c(lambda e: e.memset(thrneg[:], -1e29))
        pool_c(lambda e: e.memset(negbig16[:], -1e30))
        for i in range(8):
            v = (500000.0 ** (-(2.0 * i) / 16.0)) / (2 * math.pi)
            pool_c(lambda e, i=i, v=v: e.memset(inv2pi[:, i:i + 1], v))
        for k in range(NBIS + 1):
            pool_c(lambda e, k=k: e.memset(pow2[:, k:k + 1], 2.0 ** (-k)))

        hT = sb("hT", [128, 8, S], BF16)
        HT = P.buf("hT")
        hTflat = hT[:].rearrange("p c t -> p (c t)")
        qkg = sb("qkg", [128, 384], F32)
        GB = P.buf("gparams")
        WCOLS = 512
        wst = sb("wst", [128, 8, WCOLS], F32)
        WST = P.buf("wst")
        wbf = sb("wbf", [128, 8, WCOLS], BF16)
        WBF = P.buf("wbf")
        NB16 = 28800
        NF32 = 6144
        arena = sb("arena", [128, NB16], BF16)
        arenaF = sb("arenaF", [128, NF32], F32)
        junk8 = sb("junk8", [128, S], mybir.dt.uint8)
        JK8 = P.buf("junk8")

        def mk_rot(name, n, shape, dt):
            return Rot([(sb("%s%d" % (name, i), shape, dt)[:], P.buf(name)) for i in range(n)])

        qk_rot = mk_rot("qksb", 2, [128, 512], F32)
        sq_rot = mk_rot("sqsb", 1, [128, 512], F32)
        ss_rot = mk_rot("ss", 4, [128, 8], F32)
        rs_rot = mk_rot("rs", 4, [128, 8], F32)
        qb_rot = mk_rot("qb", 2, [128, 512], BF16)
        ra_rot = mk_rot("ra", 2, [128, 8, 2, 8], F32)
        rb_rot = mk_rot("rb", 2, [128, 8, 2, 8], F32)
        pt_rot = mk_rot("pt", 3, [128, 512], BF16)
        oas_rot = mk_rot("oas", 3, [128, 2, 65], F32)
        rc4_rot = mk_rot("rc4", 3, [128, 4], F32)
        qkh_rot = Rot([(t[:, h * 256:(h + 1) * 256], P.buf("qkh")) for t, _ in qk_rot.items for h in range(2)])
        sqh_rot = Rot([(t[:, h * 256:(h + 1) * 256], P.buf("sqh")) for t, _ in sq_rot.items for h in range(2)])
        qbh_rot = Rot([(t[:, h * 256:(h + 1) * 256], P.buf("qbh")) for t, _ in qb_rot.items for h in range(2)])
        pjp_rot = Rot([(t[:], b) for t, b in PSB[0:4]])
        obs4_rot = mk_rot("obs4", 2, [128, 4, 128], BF16)
        dstrip = sb("dstrip", [128, 7 * 128], BF16)

        posi = sb("posi", [128, 3 * NT], I32)
        posf = sb("posf", [128, 3 * NT], F32)
        ru = arenaF[:, 0:768].rearrange("p (t i) -> p t i", i=8)
        rf = arenaF[:, 768:1536].rearrange("p (t i) -> p t i", i=8)
        rk_ap = arenaF[:, 1536:2304].bitcast(I32).rearrange("p (t i) -> p t i", i=8)
        RB = P.buf("rope")
        P.dma(posi[:], posp[:, :], RB, writes=[RB])
        P.op("dve", lambda e: e.tensor_copy(out=posf[:], in_=posi[:]), reads=[RB], writes=[RB])
        for tab, shift in ((sinT, 0.0), (cosT, 0.25)):
            P.op("dve", lambda e: e.tensor_tensor(out=ru, in0=posf[:].unsqueeze(2).to_broadcast([128, 3 * NT, 8]),
                                                  in1=inv2pi[:].unsqueeze(1).to_broadcast([128, 3 * NT, 8]), op=ALU.mult),
                 reads=[RB, CONST], writes=[RB])
            if shift:
                P.op("dve", lambda e, s_=shift: e.tensor_scalar(out=ru, in0=ru, scalar1=s_, scalar2=None, op0=ALU.add),
                     reads=[RB], writes=[RB])
            P.op("dve", lambda e: e.tensor_copy(out=rk_ap, in_=ru), reads=[RB], writes=[RB])
            P.op("dve", lambda e: e.tensor_copy(out=rf, in_=rk_ap), reads=[RB], writes=[RB])
            P.op("dve", lambda e: e.tensor_tensor(out=ru, in0=ru, in1=rf, op=ALU.subtract), reads=[RB], writes=[RB])
            P.op("dve", lambda e: e.scalar_tensor_tensor(out=rf, in0=ru, scalar=0.5, in1=ru, op0=ALU.is_gt, op1=ALU.subtract),
                 reads=[RB], writes=[RB])
            P.op("act", lambda e, tab=tab: e.activation(out=tab[:], in_=rf, func=AF.Sin, scale=-2 * math.pi),
                 reads=[RB], writes=[RB, CONST])
        P.barrier()

        wq = {"list": [], "pos": 0, "loaded": -1}

        def _issue_load(idx):
            l_, segs = wq["list"][idx]
            off = 0
            for (c0, n) in segs:
                P.dma(wst[:, :, off:off + n], win_d[l_, :, c0:c0 + n].rearrange("(c p) n -> p c n", p=128), WST,
                      writes=[WST])
                off += n
            wq["loaded"] = idx

        def load_weights(l, segs):
            idx = wq["pos"]
            assert wq["list"][idx] == (l, segs), (wq["list"][idx], (l, segs))
            wq["pos"] += 1
            if wq["loaded"] < idx:
                _issue_load(idx)
            tot = sum(n for _, n in segs)
            P.op("act", lambda e, tot=tot: e.activation(out=wbf[:, :, 0:tot], in_=wst[:, :, 0:tot], func=AF.Copy),
                 reads=[WST], writes=[WBF])
            if idx + 1 < len(wq["list"]):
                _issue_load(idx + 1)
            return tot

        def cast_weights(tot):
            return None

        def project(g, tile, c0, n, pj, pjb):
            sl = tok_slice(g, tile)
            for c in range(8):
                P.op("pe", lambda e, c=c: e.matmul(pj[:, 0:n], lhsT=hT[:, c, sl], rhs=wbf[:, c, c0:c0 + n],
                                                  start=(c == 0), stop=(c == 7)),
                     reads=[HT, WBF], writes=[pjb] if c in (0, 7) else [], inc=(c == 7))

        def rope(src, srcb, dst, dstb, nu, ctile, eng="pool", eng2=None):
            eng2 = eng2 or eng
            ra, rab = ra_rot.next()
            rb, rbb = rb_rot.next()
            x12 = src.rearrange("p (u d) -> p u d", u=nu)[:, :, 0:16].rearrange("p u (h d) -> p u h d", h=2)
            cosb = cosT[:, ctile:ctile + 1, :].unsqueeze(1).to_broadcast([128, nu, 2, 8])
            sinb = sinT[:, ctile:ctile + 1, :].unsqueeze(1).to_broadcast([128, nu, 2, 8])
            P.op(eng, lambda e: e.tensor_tensor(out=ra[:, 0:nu], in0=x12, in1=cosb, op=ALU.mult),
                 reads=[srcb, CONST], writes=[rab])
            P.op(eng, lambda e: e.tensor_tensor(out=rb[:, 0:nu], in0=x12, in1=sinb, op=ALU.mult),
                 reads=[srcb, CONST], writes=[rbb])
            d3 = dst.rearrange("p (u d) -> p u d", u=nu)
            P.op(eng2, lambda e: e.tensor_tensor(out=d3[:, :, 0:8], in0=ra[:, 0:nu, 0, :], in1=rb[:, 0:nu, 1, :], op=ALU.subtract),
                 reads=[rab, rbb], writes=[dstb])
            P.op(eng2, lambda e: e.tensor_tensor(out=d3[:, :, 8:16], in0=ra[:, 0:nu, 1, :], in1=rb[:, 0:nu, 0, :], op=ALU.add),
                 reads=[rab, rbb], writes=[dstb])

        def run_groups(groups):
            prev = None
            for gi in range(len(groups) + 1):
                cur = None
                if gi < len(groups):
                    gd = groups[gi]
                    for f in gd.get("before", ()):
                        f()
                    stt, stb = st_rot.next()
                    ptt, ptb = pt_rot.next()
                    nq = len(gd["qk"])
                    for qi, (fn, rds) in enumerate(gd["qk"]):
                        P.op("pe", lambda e, fn=fn, stt=stt: fn(e, stt), reads=rds,
                             writes=[stb] if qi in (0, nq - 1) else [], inc=(qi == nq - 1))
                    n = gd["n"]
                    P.op("act", lambda e, stt=stt, ptt=ptt, n=n: e.activation(out=ptt[:, 0:n], in_=stt[:, 0:n], func=AF.Exp,
                                                                               scale=ATT_SCALE),
                         reads=[stb], writes=[ptb])
                    cur = (gd, ptt, ptb)
                if prev is not None:
                    gd, ptt, ptb = prev
                    for (oz, ozb) in gd.get("pre", ()):
                        P.op("dve", lambda e, oz=oz: e.memset(oz[:, 0:65], 0.0), writes=[ozb])
                    for (fn, rds, ob) in gd["pv"]:
                        P.op("pe", lambda e, fn=fn, ptt=ptt: fn(e, ptt), reads=[ptb] + rds, writes=[ob], inc=True)
                    for f in gd["after"]:
                        f()
                prev = cur

        QT = [arena[:, 0:4096], arena[:, 4096:8192]]
        KT = [arena[:, 8192:12288], arena[:, 12288:16384]]
        QKB = P.buf("qkt")
        VA = arena[:, 16384:16384 + NT * 130].rearrange("p (t h d) -> p t h d", t=NT, h=2)
        VB = P.buf("v")
        A_END = 16384 + NT * 130

        EINIT = P.buf("einit")

        def init_qk_tiles(moba):
            P.op("pool", lambda e: e.memset(KT[0][64:128, :], 0.0), reads=[QKB], writes=[QKB])
            P.op("pool", lambda e: e.memset(KT[1][0:64, :], 0.0), reads=[QKB], writes=[QKB])
            if moba:
                P.op("pool", lambda e: e.memset(QT[0][64:128, :], 0.0), reads=[QKB], writes=[QKB])
                P.op("pool", lambda e: e.memset(QT[1][0:64, :], 0.0), reads=[QKB], writes=[QKB])
                P.op("pool", lambda e: e.memset(KT[1][0:16, :], BIG), reads=[QKB], writes=[QKB])
                P.op("pool", lambda e: e.affine_select(out=KT[1][0:16, :], in_=KT[1][0:16, :], pattern=[[1, S]], compare_op=ALU.is_ge,
                                                       fill=0.0, base=0, channel_multiplier=-256), reads=[QKB], writes=[QKB])
                P.op("pool", lambda e: e.affine_select(out=KT[1][0:16, :], in_=KT[1][0:16, :], pattern=[[-1, S]], compare_op=ALU.is_ge,
                                                       fill=0.0, base=255, channel_multiplier=256), reads=[QKB], writes=[QKB])
                P.dma(KT[0][64:80, :], KT[1][0:16, :], EINIT, reads=[QKB], writes=[QKB])

        def prep_pair(l, g, q0, k0, v0, mixer, moba=False):
            tot = load_weights(l, [(q0, 128), (k0, 128), (v0, 128)])
            cast_weights(tot)
            P.op("pool", lambda e: e.memset(VA[:, :, :, 64:65], 1.0), reads=[VB], writes=[VB])
            LA = 3
            pjs = {}

            def stage1(t):
                pjs[t] = pjp_rot.next()
                project(g, t, 0, 384, pjs[t][0], pjs[t][1])
            st2 = {}

            stc = {}

            def stage2c(tile):
                pj, pjb = pjs.pop(tile)
                qs, qsb = qkh_rot.next()
                stc[tile] = (qs, qsb)
                P.op("act", lambda e, qs=qs, pj=pj: e.activation(out=qs[:, 0:256], in_=pj[:, 0:256], func=AF.Copy),
                     reads=[pjb], writes=[qsb])
                P.op("act", lambda e, pj=pj, tile=tile: e.activation(out=VA[:, tile, :, 0:64],
                                                                     in_=pj[:, 256:384].rearrange("p (h d) -> p h d", h=2),
                                                                     func=AF.Copy),
                     reads=[pjb], writes=[VB])

            def stage2a(tile):
                qs, qsb = stc.pop(tile)
                sq, sqb = sqh_rot.next()
                ss, ssb = ss_rot.next()
                rs, rsb = rs_rot.next()
                st2[tile] = (qs, qsb, rs, rsb)
                P.op("dve", lambda e, qs=qs, sq=sq: e.tensor_tensor(out=sq[:, 0:256], in0=qs[:, 0:256], in1=qs[:, 0:256], op=ALU.mult),
                     reads=[qsb], writes=[sqb])
                P.op("dve", lambda e, sq=sq, ss=ss: e.tensor_reduce(out=ss[:, 0:4], in_=sq[:, 0:256].rearrange("p (u d) -> p u d", u=4),
                                                                    axis=AX.X, op=ALU.add),
                     reads=[sqb], writes=[ssb])
                P.op("pool", lambda e, ss=ss: e.tensor_scalar(out=ss[:, 0:4], in0=ss[:, 0:4], scalar1=1.0 / HD, scalar2=EPS,
                                                              op0=ALU.mult, op1=ALU.add), reads=[ssb], writes=[ssb])
                P.op("pool", lambda e, ss=ss, rs=rs: e.tensor_tensor(out=rs[:, 0:4], in0=ss[:, 0:4], in1=neghalf[:, 0:4], op=ALU.pow),
                     reads=[ssb, CONST], writes=[rsb])

            def stage2b(tile):
                qs, qsb, rs, rsb = st2.pop(tile)
                qb, qbb = qbh_rot.next()
                for u in range(4):
                    gsl = qkg[:, mixer * 128 + (u // 2) * 64:mixer * 128 + (u // 2) * 64 + 64]
                    P.op("dve", lambda e, u=u, qs=qs, qb=qb, rs=rs, gsl=gsl: e.scalar_tensor_tensor(
                        out=qb[:, u * 64:(u + 1) * 64], in0=qs[:, u * 64:(u + 1) * 64], scalar=rs[:, u:u + 1], in1=gsl,
                        op0=ALU.mult, op1=ALU.mult), reads=[qsb, rsb, GB], writes=[qbb])
                rope(qb[:, 0:256], qbb, qb[:, 0:256], qbb, 4, g * NT + tile, eng="dve", eng2="pool")
                tp, tpb = tp_rot.next()
                for a in range(2):
                    P.op("pe", lambda e, a=a, tp=tp, qb=qb: e.transpose(out=tp[:, a, :], in_=qb[:, a * 128:(a + 1) * 128], identity=ident[:]),
                         reads=[qbb, CONST], writes=[tpb], inc=(a == 1))
                cs_ = slice(tile * 128, (tile + 1) * 128)
                if moba:
                    P.op("act", lambda e, tp=tp, cs_=cs_: e.activation(out=QT[0][0:64, cs_], in_=tp[0:64, 0, :], func=AF.Copy),
                         reads=[tpb], writes=[QKB])
                    P.op("act", lambda e, tp=tp, cs_=cs_: e.activation(out=QT[1][64:128, cs_], in_=tp[64:128, 0, :], func=AF.Copy),
                         reads=[tpb], writes=[QKB])
                else:
                    P.op("act", lambda e, tp=tp, cs_=cs_: e.activation(out=QT[0][:, cs_], in_=tp[:, 0, :], func=AF.Copy),
                         reads=[tpb], writes=[QKB])
                P.op("act", lambda e, tp=tp, cs_=cs_: e.activation(out=KT[0][0:64, cs_], in_=tp[0:64, 1, :], func=AF.Copy),
                     reads=[tpb], writes=[QKB])
                P.op("act", lambda e, tp=tp, cs_=cs_: e.activation(out=KT[1][64:128, cs_], in_=tp[64:128, 1, :], func=AF.Copy),
                     reads=[tpb], writes=[QKB])

            for t in range(min(LA, NT)):
                stage1(t)
            stage2c(0)
            stage2c(1)
            stage2a(0)
            for tile in range(NT):
                if tile + LA < NT:
                    stage1(tile + LA)
                if tile + 2 < NT:
                    stage2c(tile + 2)
                if tile + 1 < NT:
                    stage2a(tile + 1)
                stage2b(tile)

        for l in range(L):
            xsrc = x_in if l == 0 else x1_d
            XS = P.buf("xsrc")
            xdst = out_d if l == L - 1 else x1_d
            P.dma(qkg[:], qkg_d[l:l + 1, :].partition_broadcast(128), GB, writes=[GB])
            units = []
            if "a" in phases:
                import os
                for g in range(int(os.environ.get("DBG_G0", "0")), int(os.environ.get("DBG_G1", "3"))):
                    for pp in range(int(os.environ.get("DBG_PP", "4"))):
                        hq = (g * 8 + 2 * pp) * HD
                        units.append((l, [(A_OFF + hq, 128), (A_OFF + 1536 + hq, 128), (A_OFF + 3072 + hq, 128)]))
            if "idx" in phases:
                units.append((l, [(IQ_OFF, 512)]))
                units.append((l, [(IK_OFF, 72)]))
            for ph, off in (("b", B_OFF), ("c", C_OFF)):
                if ph in phases:
                    for pp in range(4):
                        hq = 2 * pp * HD
                        units.append((l, [(off + hq, 128), (off + 512 + hq, 128), (off + 1024 + hq, 128)]))
            if "z" in phases:
                for blk in range(9):
                    units.append((l, [(Z_OFF + blk * 512, 512)]))
            wq["list"] = wq["list"][:wq["pos"]] + units

            if "p0" in phases:
                gnorm = arenaF[:, 2048:3072]
                P.dma(gnorm, normg_d[l:l + 1, :].partition_broadcast(128), GB, writes=[GB])
                xt_rot = Rot([(arenaF[:, i * 1024:(i + 1) * 1024], P.buf("xt")) for i in range(2)])
                hb_rot = Rot([(arena[:, i * 1024:(i + 1) * 1024], P.buf("hb")) for i in range(2)])
                junk = arena[:, 2048:3072]
                JK = P.buf("junk")
                p0st = {}

                def p0_a(tt):
                        xt, xtb = xt_rot.next()
                        hb, hbb = hb_rot.next()
                        ss, ssb = ss_rot.next()
                        rs, rsb = rs_rot.next()
                        P.dma(xt, xsrc[tt * 128:(tt + 1) * 128, :], xtb, reads=[XS], writes=[xtb])
                        P.op("act", lambda e, xt=xt, ss=ss: e.activation(out=junk, in_=xt, func=AF.Square, accum_out=ss[:, 0:1]),
                             reads=[xtb], writes=[JK, ssb])
                        P.op("pool", lambda e, ss=ss: e.tensor_scalar(out=ss[:, 0:1], in0=ss[:, 0:1], scalar1=1.0 / D, scalar2=EPS,
                                                                      op0=ALU.mult, op1=ALU.add), reads=[ssb], writes=[ssb])
                        P.op("pool", lambda e, ss=ss, rs=rs: e.tensor_tensor(out=rs[:, 0:1], in0=ss[:, 0:1], in1=neghalf[:, 0:1], op=ALU.pow),
                             reads=[ssb, CONST], writes=[rsb])
                        P.op("dve", lambda e, xt=xt, rs=rs, hb=hb: e.scalar_tensor_tensor(out=hb, in0=xt, scalar=rs[:, 0:1], in1=gnorm,
                                                                                          op0=ALU.mult, op1=ALU.mult),
                             reads=[xtb, rsb, GB], writes=[hbb])
                        p0st[tt] = (hb, hbb)

                def p0_b(tt):
                        hb, hbb = p0st.pop(tt)
                        for cc in range(8):
                            P.op("pe", lambda e, cc=cc, hb=hb: e.transpose(out=TPB[:, cc, :], in_=hb[:, cc * 128:(cc + 1) * 128],
                                                                          identity=ident[:]),
                                 reads=[hbb, CONST], writes=TB if cc in (0, 7) else [], inc=(cc == 7))
                        P.op("act", lambda e, tt=tt: e.activation(out=hT[:, :, tt * 128:(tt + 1) * 128], in_=TPB[:, :, :], func=AF.Copy),
                             reads=TB, writes=[HT])

                p0_a(0)
                for tt in range(NT):
                    if tt + 1 < NT:
                        p0_a(tt + 1)
                    p0_b(tt)
                P.barrier()

            if "a" in phases:
                import os
                init_qk_tiles(False)
                for g in range(int(os.environ.get("DBG_G0", "0")), int(os.environ.get("DBG_G1", "3"))):
                    d = DILS[g]
                    nb = NT // d
                    for pp in range(int(os.environ.get("DBG_PP", "4"))):
                        hq = (g * 8 + 2 * pp) * HD
                        prep_pair(l, g, A_OFF + hq, A_OFF + 1536 + hq, A_OFF + 3072 + hq, 0)
                        groups = []
                        for r in range(d):
                            for j in range(nb):
                                gt = r * nb + j
                                oas_slot = oas_rot.next()
                                for hh in range(2):
                                    base = hh * 64
                                    oj, ojb = o_rot.next()
                                    kts = ([(gt - 1, 128)] if j > 0 else []) + [(gt, 0)]
                                    qk = []
                                    pv = []
                                    for jj, (kt, boff) in enumerate(kts):
                                        def qk1(e, stt, hh=hh, kt=kt, gt=gt, jj=jj):
                                            return e.matmul(stt[:, jj * 128:(jj + 1) * 128], lhsT=KT[hh][:, kt * 128:(kt + 1) * 128],
                                                            rhs=QT[0][:, gt * 128:(gt + 1) * 128], start=True, stop=False)

                                        def qk2(e, stt, jj=jj, boff=boff):
                                            return e.matmul(stt[:, jj * 128:(jj + 1) * 128], lhsT=identB[:], rhs=band[:, boff:boff + 128],
                                                            start=False, stop=True)
                                        qk.append((qk1, [QKB]))
                                        qk.append((qk2, [CONST]))

                                        def pvf(e, ptt, oj=oj, kt=kt, jj=jj, hh=hh, nk=len(kts)):
                                            return e.matmul(oj[:, 0:65], lhsT=ptt[:, jj * 128:(jj + 1) * 128], rhs=VA[:, kt, hh, :],
                                                            start=(jj == 0), stop=(jj == nk - 1))
                                        pv.append((pvf, [VB], ojb))

                                    def evac(oj=oj, ojb=ojb, hh=hh, gt=gt, g=g, pp=pp, oas_slot=oas_slot):
                                        oas, oasb = oas_slot
                                        P.op("dve", lambda e: e.tensor_copy(out=oas[:, hh, :], in_=oj[:, 0:65]), reads=[ojb], writes=[oasb])
                                        if hh == 1:
                                            P.dma(oa_d[tok_slice(g, gt), g, 2 * pp:2 * pp + 2, :], oas, oasb, reads=[oasb])
                                    groups.append({"qk": qk, "n": 128 * len(kts), "pv": pv, "after": [evac]})
                        if os.environ.get("DBG_NOATT") is None:
                            run_groups(groups)
                P.barrier()

            if "idx" in phases:
                QiT = arena[:, 0:16384].rearrange("p (a t) -> p a t", a=4)
                KiT2 = arena[:, 16384:20480]
                mbf = arena[:, 20480:24576]
                mbT = arena[:, 24576:28672].rearrange("p (j q) -> p j q", q=128)
                QIB, KIB, MBB, MBTB = P.buf("qit"), P.buf("kit"), P.buf("mb"), P.buf("mbT")
                acc = arenaF[:, 0:4096]
                ACC = P.buf("acc")
                tmp_rot = Rot([(arenaF[:, 4096 + i * 512:4096 + (i + 1) * 512], P.buf("tmp")) for i in range(2)])
                wsc = arenaF[:, 5120:5376].rearrange("p (t h) -> p t h", h=8)
                WSC = P.buf("wsc")
                bis = sb("bis", [128, 8 + 2 * (NBIS + 1)], F32) if l == 0 else bis
                BIS = P.buf("bis")
                lo, hi, rng_, mid, cntv, dd, sga = (bis[:, i:i + 1] for i in range(7))
                MIDB, CNTB, SGAB, JK8A = P.buf("mid"), P.buf("cnt"), P.buf("sga"), P.buf("jk8a")
                steps2 = bis[:, 8:8 + NBIS + 1]
                steps = bis[:, 8 + NBIS + 1:8 + 2 * (NBIS + 1)]

                for blk in range(2):
                    if blk == 0:
                        tot = load_weights(l, [(IQ_OFF, 512)])
                    else:
                        tot = load_weights(l, [(IK_OFF, 72)])
                    cast_weights(tot)
                    for tile in range(NT):
                        pj, pjb = pj_rot.next()
                        project(0, tile, 0, tot, pj, pjb)
                        qs, qsb = qk_rot.next()
                        qb, qbb = qb_rot.next()
                        if blk == 0:
                            P.op("act", lambda e, qs=qs, pj=pj: e.activation(out=qs[:, 0:512], in_=pj[:, 0:512], func=AF.Copy, scale=0.125),
                                 reads=[pjb], writes=[qsb])
                            P.op("pool", lambda e, qb=qb, qs=qs: e.tensor_copy(out=qb[:, 0:512], in_=qs[:, 0:512]), reads=[qsb], writes=[qbb])
                            rope(qs[:, 0:512], qsb, qb[:, 0:512], qbb, 8, tile)
                            for a in range(4):
                                tp, tpb = tp_rot.next() if a % 2 == 0 else (tp, tpb)
                                P.op("pe", lambda e, a=a, tp=tp, qb=qb: e.transpose(out=tp[:, a % 2, :], in_=qb[:, a * 128:(a + 1) * 128],
                                                                                    identity=ident[:]),
                                     reads=[qbb, CONST], writes=[tpb], inc=(a % 2 == 1))
                                if a % 2 == 1:
                                    P.op("act", lambda e, tp=tp, tile=tile, a=a: e.activation(
                                        out=QiT[:, a - 1:a + 1, tile * 128:(tile + 1) * 128], in_=tp, func=AF.Copy),
                                        reads=[tpb], writes=[QIB])
                        else:
                            P.op("act", lambda e, qs=qs, pj=pj: e.activation(out=qs[:, 0:64], in_=pj[:, 0:64], func=AF.Copy),
                                 reads=[pjb], writes=[qsb])
                            P.op("act", lambda e, pj=pj, tile=tile: e.activation(out=wsc[:, tile, :], in_=pj[:, 64:72], func=AF.Copy,
                                                                                 scale=8.0 ** -0.5),
                                 reads=[pjb], writes=[WSC])
                            P.op("pool", lambda e, qb=qb, qs=qs: e.tensor_copy(out=qb[:, 0:64], in_=qs[:, 0:64]), reads=[qsb], writes=[qbb])
                            rope(qs[:, 0:64], qsb, qb[:, 0:64], qbb, 1, tile)
                            P.op("pool", lambda e, qb=qb: e.tensor_copy(out=qb[:, 64:128], in_=qb[:, 0:64]), reads=[qbb], writes=[qbb])
                            tp, tpb = tp_rot.next()
                            P.op("pe", lambda e, tp=tp, qb=qb: e.transpose(out=tp[:, 0, :], in_=qb[:, 0:128], identity=ident[:]),
                                 reads=[qbb, CONST], writes=[tpb])
                            P.op("act", lambda e, tp=tp, tile=tile: e.activation(out=KiT2[:, tile * 128:(tile + 1) * 128], in_=tp[:, 0, :],
                                                                                 func=AF.Copy),
                                 reads=[tpb], writes=[KIB])

                for i in range(NT):
                    sv = (i + 1) * 128
                    for sc in range((sv + 511) // 512):
                        ncol = min(512, sv - sc * 512)
                        for h in range(8):
                            base = (h % 2) * 64
                            stt, stb = st_rot.next()
                            tmp, tmpb = tmp_rot.next()
                            P.op("pe", lambda e, stt=stt, base=base, h=h, i=i, sc=sc, ncol=ncol: e.matmul(
                                stt[:, 0:ncol], lhsT=QiT[base:base + 64, h // 2, i * 128:(i + 1) * 128],
                                rhs=KiT2[base:base + 64, sc * 512:sc * 512 + ncol], start=True, stop=True),
                                reads=[QIB, KIB], writes=[stb])
                            P.op("act", lambda e, stt=stt, tmp=tmp, ncol=ncol: e.activation(out=tmp[:, 0:ncol], in_=stt[:, 0:ncol], func=AF.Relu),
                                 reads=[stb], writes=[tmpb])
                            a_sl = acc[:, sc * 512:sc * 512 + ncol]
                            if h == 0:
                                P.op("dve", lambda e, a_sl=a_sl, tmp=tmp, ncol=ncol, i=i, h=h: e.tensor_scalar(
                                    out=a_sl, in0=tmp[:, 0:ncol], scalar1=wsc[:, i, h:h + 1], scalar2=None, op0=ALU.mult),
                                    reads=[tmpb, WSC], writes=[ACC])
                            else:
                                P.op("dve", lambda e, a_sl=a_sl, tmp=tmp, ncol=ncol, i=i, h=h: e.scalar_tensor_tensor(
                                    out=a_sl, in0=tmp[:, 0:ncol], scalar=wsc[:, i, h:h + 1], in1=a_sl, op0=ALU.mult, op1=ALU.add),
                                    reads=[tmpb, WSC, ACC], writes=[ACC])
                    if i >= 2:
                        P.op("dve", lambda e, sv=sv: e.tensor_reduce(out=lo, in_=acc[:, 0:sv], axis=AX.X, op=ALU.min),
                             reads=[ACC], writes=[BIS])
                    dsl = acc[:, i * 128:(i + 1) * 128]
                    P.op("dve", lambda e, dsl=dsl: e.tensor_tensor(out=dsl, in0=dsl, in1=triq[:], op=ALU.add),
                         reads=[ACC, CONST], writes=[ACC])
                    if i >= 2:
                        P.op("dve", lambda e, sv=sv: e.tensor_reduce(out=hi, in_=acc[:, 0:sv], axis=AX.X, op=ALU.max),
                             reads=[ACC], writes=[BIS])
                        P.op("dve", lambda e: e.tensor_tensor(out=rng_, in0=hi, in1=lo, op=ALU.subtract), reads=[BIS], writes=[BIS])
                        P.op("dve", lambda e: e.tensor_scalar(out=steps2, in0=pow2[:, 0:NBIS + 1], scalar1=rng_, scalar2=None, op0=ALU.mult),
                             reads=[BIS, CONST], writes=[BIS])
                        P.op("dve", lambda e: e.tensor_scalar(out=steps, in0=steps2, scalar1=0.5, scalar2=None, op0=ALU.mult),
                             reads=[BIS], writes=[BIS])
                        P.op("dve", lambda e: e.tensor_tensor(out=mid, in0=lo, in1=steps[:, 0:1], op=ALU.add), reads=[BIS], writes=[MIDB])
                        h1 = max(1, int(0.42 * (i + 1) + 0.5)) * 128
                        n2 = sv - h1
                        thr_cnt = float(TOPK) - 0.5 * n2
                        for k in range(NBIS):
                            P.op("dve", lambda e, h1=h1: e.tensor_scalar(out=junk8[:, 0:h1], in0=acc[:, 0:h1], scalar1=mid, scalar2=None,
                                                                        op0=ALU.is_ge, op1=ALU.add, accum_out=cntv),
                                 reads=[ACC, MIDB], writes=[JK8, CNTB])
                            P.op("act", lambda e, h1=h1, sv=sv: e.activation(out=junk8[:, h1:sv], in_=acc[:, h1:sv], func=AF.Sign,
                                                                             bias=mid, scale=-1.0, accum_out=sga),
                                 reads=[ACC, MIDB], writes=[JK8A, SGAB])
                            P.op("dve", lambda e: e.scalar_tensor_tensor(out=cntv, in0=sga, scalar=-0.5, in1=cntv, op0=ALU.mult, op1=ALU.add),
                                 reads=[SGAB, CNTB], writes=[CNTB])
                            if k < NBIS - 1:
                                P.op("dve", lambda e, k=k, thr_cnt=thr_cnt: e.tensor_scalar(out=dd, in0=cntv, scalar1=thr_cnt,
                                                                                            scalar2=steps2[:, k + 1:k + 2],
                                                                                            op0=ALU.is_ge, op1=ALU.mult), reads=[CNTB, BIS], writes=[BIS])
                                P.op("dve", lambda e, k=k: e.scalar_tensor_tensor(out=mid, in0=dd, scalar=steps[:, k + 1:k + 2], in1=mid,
                                                                                  op0=ALU.subtract, op1=ALU.add), reads=[BIS, MIDB], writes=[MIDB])
                            else:
                                P.op("dve", lambda e, thr_cnt=thr_cnt: e.tensor_scalar(out=dd, in0=cntv, scalar1=thr_cnt, scalar2=-1.0,
                                                                                       op0=ALU.is_ge, op1=ALU.add), reads=[CNTB, BIS], writes=[BIS])
                                P.op("dve", lambda e, k=k: e.scalar_tensor_tensor(out=mid, in0=dd, scalar=steps[:, k:k + 1], in1=mid,
                                                                                  op0=ALU.mult, op1=ALU.add), reads=[BIS, MIDB], writes=[MIDB])
                        thr = mid
                    else:
                        thr = thrneg[:, 0:1]
                    P.op("dve", lambda e, sv=sv, thr=thr: e.tensor_scalar(out=mbf[:, 0:sv], in0=acc[:, 0:sv], scalar1=thr, scalar2=1.0,
                                                                          op0=ALU.is_ge, op1=ALU.subtract),
                         reads=[ACC, BIS, MIDB, CONST], writes=[MBB])
                    for j0 in range(0, i + 1, 8):
                        nj = min(8, i + 1 - j0)
                        for jj in range(nj):
                            j = j0 + jj
                            P.op("pe", lambda e, jj=jj, j=j: e.transpose(out=TPB[:, jj, :], in_=mbf[:, j * 128:(j + 1) * 128], identity=ident[:]),
                                 reads=[MBB, CONST], writes=TB if jj in (0, nj - 1) else [], inc=(jj == nj - 1))
                        P.op("act", lambda e, j0=j0, nj=nj: e.activation(out=mbT[:, j0:j0 + nj, :], in_=TPB[:, 0:nj, :], func=AF.Copy),
                             reads=TB, writes=[MBTB])
                    cch, u = i // 4, i % 4
                    njt = 4 * cch + 4
                    if njt > i + 1:
                        P.op("pool", lambda e, i=i, njt=njt: e.memset(mbT[:, i + 1:njt, :], -1.0), reads=[MBTB], writes=[MBTB])
                    P.dma(mk_d[cch, 0:njt, :, u * 128:(u + 1) * 128].rearrange("j s q -> s j q"), mbT[:, 0:njt, :], MBTB, reads=[MBTB])
                P.barrier()

            if "b" in phases:
                ml_rot = Rot([(arena[:, A_END + k * 512:A_END + (k + 1) * 512], P.buf("mload")) for k in range(8)])
                init_qk_tiles(False)
                for pp in range(4):
                    hq = 2 * pp * HD
                    prep_pair(l, 0, B_OFF + hq, B_OFF + 512 + hq, B_OFF + 1024 + hq, 1)
                    groups = []
                    for cch in range(8):
                        obs_slot = obs4_rot.next()
                        obank = [o4_rot.next(), o4_rot.next()]
                        nj = 4 * cch + 4
                        for j in range(nj):
                            ml, mlb = ml_rot.next()

                            def load_mask(ml=ml, mlb=mlb, cch=cch, j=j):
                                P.dma(ml, mk_d[cch, j, :, :], mlb, writes=[mlb])
                            for hh in range(2):
                                base = hh * 64
                                o4, o4b = obank[hh]

                                def qk1(e, stt, hh=hh, j=j, cch=cch):
                                    return e.matmul(stt[:, 0:512], lhsT=KT[hh][:, j * 128:(j + 1) * 128],
                                                    rhs=QT[0][:, cch * 512:(cch + 1) * 512], start=True, stop=False)

                                def qk2(e, stt, ml=ml):
                                    return e.matmul(stt[:, 0:512], lhsT=identB[:], rhs=ml, start=False, stop=True)
                                pv = []
                                for u in range(4):
                                    if j > 4 * cch + u:
                                        continue

                                    def pvf(e, ptt, o4=o4, j=j, u=u, hh=hh, cch=cch):
                                        return e.matmul(o4[:, u, 0:65], lhsT=ptt[:, u * 128:(u + 1) * 128], rhs=VA[:, j, hh, :],
                                                        start=(j == 0 and u == 0), stop=(j == 4 * cch + u), skip_group_check=True)
                                    pv.append((pvf, [VB], o4b))
                                after = []
                                if j == nj - 1:
                                    def fin(o4=o4, o4b=o4b, hh=hh, cch=cch, pp=pp, obs_slot=obs_slot):
                                        obs, obsb = obs_slot
                                        rc4, rc4b = rc4_rot.next()
                                        P.op("dve", lambda e: e.reciprocal(out=rc4, in_=o4[:, :, 64]), reads=[o4b], writes=[rc4b])
                                        P.op("dve", lambda e: e.tensor_tensor(out=obs[:, :, hh * 64:(hh + 1) * 64], in0=o4[:, :, 0:64],
                                                                              in1=rc4.unsqueeze(2).to_broadcast([128, 4, 64]), op=ALU.mult),
                                             reads=[o4b, rc4b], writes=[obsb])
                                        if hh == 1:
                                            P.dma(obc_d[4 * cch:4 * cch + 4, :, pp * 128:(pp + 1) * 128].rearrange("t p d -> p t d"), obs, obsb,
                                                  reads=[obsb], eng="pool")
                                    after.append(fin)
                                groups.append({"qk": [(qk1, [QKB]), (qk2, [CONST, mlb])], "n": 512, "pv": pv, "after": after,
                                               "before": [load_mask] if hh == 0 else []})
                    run_groups(groups)
                P.barrier()

            if "c" in phases:
                kmT = arena[:, A_END:A_END + 16]
                init_qk_tiles(True)
                KMB = P.buf("kmT")
                kmf = sb("kmf", [128, 16], F32) if l == 0 else kmf
                g16 = sb("g16", [128, 16], F32) if l == 0 else g16
                m8 = sb("m8", [128, 8], F32) if l == 0 else m8
                gA, gB, gC, gE = (arenaF[:, k * 512:(k + 1) * 512] for k in range(4))
                pmask = arenaF[:, 2048:2560]
                ownm = arenaF[:, 2560:3072]
                gm = sb("gm", [128, NT], F32) if l == 0 else gm
                sel32 = arena[:, A_END + 16:A_END + 16 + NT * 80].rearrange("p (t c) -> p t c", c=80)
                GSB, GMB, GEB, PMB = P.buf("gs"), P.buf("gm"), P.buf("ge"), P.buf("pm")
                pm4 = pmask.rearrange("p (o t b) -> p o t b", o=16, t=2)
                ow4 = ownm.rearrange("p (o t b) -> p o t b", o=16, t=2)
                P.op("pool", lambda e: e.memset(pmask, 0.0), writes=[PMB])
                P.op("pool", lambda e: e.affine_select(out=pm4, in_=pm4, pattern=[[1, 16], [0, 2], [-1, 16]], compare_op=ALU.is_ge,
                                                       fill=-1e30, base=-1, channel_multiplier=0), reads=[PMB], writes=[PMB])
                P.op("pool", lambda e: e.memset(ownm, 1.0), reads=[PMB], writes=[PMB])
                P.op("pool", lambda e: e.affine_select(out=ow4, in_=ow4, pattern=[[1, 16], [0, 2], [-1, 16]], compare_op=ALU.is_equal,
                                                       fill=0.0, base=0, channel_multiplier=0), reads=[PMB], writes=[PMB])
                S16 = P.buf("sel32")
                P.op("pool", lambda e: e.memset(sel32, 0.0), writes=[S16])
                for pp in range(4):
                    hq = 2 * pp * HD
                    prep_pair(l, 0, C_OFF + hq, C_OFF + 512 + hq, C_OFF + 1024 + hq, 2, moba=True)
                    P.op("dve", lambda e: e.tensor_reduce(out=kmf[0:64, :], in_=KT[0][0:64, :].rearrange("p (b k) -> p b k", k=256), axis=AX.X, op=ALU.add),
                         reads=[QKB], writes=[KMB])
                    P.op("dve", lambda e: e.tensor_reduce(out=kmf[64:128, :], in_=KT[1][64:128, :].rearrange("p (b k) -> p b k", k=256), axis=AX.X, op=ALU.add),
                         reads=[QKB], writes=[KMB])
                    P.op("dve", lambda e: e.tensor_scalar(out=kmT, in0=kmf[:], scalar1=1.0 / 256, scalar2=None, op0=ALU.mult),
                         reads=[KMB], writes=[KMB])
                    for hh in range(2):
                        base = hh * 64
                        c0 = 64 if hh == 0 else 0
                        nr = c0 + 16
                        gp, gpb = pj_rot.next()
                        for i in range(NT):
                            P.op("pe", lambda e, gp=gp, base=base, i=i, hh=hh: e.matmul(gp[:, i * 16:(i + 1) * 16],
                                                                                       lhsT=QT[hh][base:base + 64, i * 128:(i + 1) * 128],
                                                                                       rhs=kmT[base:base + 64, :], start=True, stop=True),
                                 reads=[QKB, KMB], writes=[gpb] if i in (0, NT - 1) else [], inc=(i == NT - 1))
                        P.op("dve", lambda e, gp=gp: e.tensor_tensor(out=gA, in0=gp[:, 0:512], in1=pmask, op=ALU.add),
                             reads=[gpb, PMB], writes=[GSB])
                        src = gA
                        for rnd, dst in enumerate((gB, gC, None)):
                            P.op("dve", lambda e, src=src: e.tensor_reduce(out=gm[:], in_=src.rearrange("p (t b) -> p t b", b=16), axis=AX.X, op=ALU.max),
                                 reads=[GSB], writes=[GMB])
                            if dst is None:
                                break
                            P.op("dve", lambda e, src=src: e.tensor_tensor(out=gE.rearrange("p (t b) -> p t b", b=16),
                                                                           in0=src.rearrange("p (t b) -> p t b", b=16),
                                                                           in1=gm[:].unsqueeze(2).to_broadcast([128, NT, 16]), op=ALU.is_ge),
                                 reads=[GSB, GMB], writes=[GEB])
                            P.op("dve", lambda e, src=src, dst=dst: e.scalar_tensor_tensor(out=dst, in0=gE, scalar=-1e30, in1=src, op0=ALU.mult, op1=ALU.add),
                                 reads=[GSB, GEB], writes=[GSB])
                            src = dst
                        P.op("dve", lambda e: e.tensor_tensor(out=gE.rearrange("p (t b) -> p t b", b=16), in0=gA.rearrange("p (t b) -> p t b", b=16),
                                                              in1=gm[:].unsqueeze(2).to_broadcast([128, NT, 16]), op=ALU.is_ge),
                             reads=[GSB, GMB], writes=[GEB])
                        P.op("dve", lambda e: e.tensor_tensor(out=gE, in0=gE, in1=ownm, op=ALU.max), reads=[GEB, PMB], writes=[GEB])
                        P.op("dve", lambda e, c0=c0: e.tensor_scalar(out=sel32[:, :, c0:c0 + 16], in0=gE.rearrange("p (t b) -> p t b", b=16),
                                                                     scalar1=-1.0, scalar2=None, op0=ALU.add),
                             reads=[GEB], writes=[S16])
                        for i0 in range(0, NT, 8):
                            for jj in range(8):
                                P.op("pe", lambda e, jj=jj, i0=i0, nr=nr: e.transpose(out=TPB[0:nr, jj, :], in_=sel32[:, i0 + jj, 0:nr], identity=ident[:]),
                                     reads=[S16, CONST], writes=TB if jj in (0, 7) else [], inc=(jj == 7))
                            P.op("act", lambda e, hh=hh, i0=i0, c0=c0: e.activation(
                                out=QT[hh][c0:c0 + 16, i0 * 128:(i0 + 8) * 128].rearrange("p (j q) -> p j q", q=128),
                                in_=TPB[c0:c0 + 16, :, :], func=AF.Copy), reads=TB, writes=[QKB])
                    groups = []
                    for cch in range(8):
                        obs_slot = obs4_rot.next()
                        obank = [o4_rot.next(), o4_rot.next()]
                        nj = 4 * cch + 4
                        for j in range(nj):
                            for hh in range(2):
                                base = hh * 64
                                o4, o4b = obank[hh]
                                qk = []

                                def qk1(e, stt, hh=hh, j=j, cch=cch):
                                    return e.matmul(stt[:, 0:512], lhsT=KT[hh][:, j * 128:(j + 1) * 128],
                                                    rhs=QT[hh][:, cch * 512:(cch + 1) * 512], start=True, stop=(j < 4 * cch))
                                qk.append((qk1, [QKB]))
                                if j >= 4 * cch:
                                    def qk3(e, stt, uj=j - 4 * cch):
                                        return e.matmul(stt[:, 0:512], lhsT=identB[:], rhs=dstrip[:, (3 - uj) * 128:(3 - uj) * 128 + 512],
                                                        start=False, stop=True)
                                    qk.append((qk3, [CONST]))
                                pv = []
                                for u in range(4):
                                    if j > 4 * cch + u:
                                        continue

                                    def pvf(e, ptt, o4=o4, j=j, u=u, hh=hh, cch=cch):
                                        return e.matmul(o4[:, u, 0:65], lhsT=ptt[:, u * 128:(u + 1) * 128], rhs=VA[:, j, hh, :],
                                                        start=(j == 0 and u == 0), stop=(j == 4 * cch + u), skip_group_check=True)
                                    pv.append((pvf, [VB], o4b))
                                after = []
                                if j == nj - 1:
                                    def fin(o4=o4, o4b=o4b, hh=hh, cch=cch, pp=pp, obs_slot=obs_slot):
                                        obs, obsb = obs_slot
                                        rc4, rc4b = rc4_rot.next()
                                        P.op("dve", lambda e: e.reciprocal(out=rc4, in_=o4[:, :, 64]), reads=[o4b], writes=[rc4b])
                                        P.op("dve", lambda e: e.tensor_tensor(out=obs[:, :, hh * 64:(hh + 1) * 64], in0=o4[:, :, 0:64],
                                                                              in1=rc4.unsqueeze(2).to_broadcast([128, 4, 64]), op=ALU.mult),
                                             reads=[o4b, rc4b], writes=[obsb])
                                        if hh == 1:
                                            P.dma(obc_d[4 * cch:4 * cch + 4, :, 512 + pp * 128:512 + (pp + 1) * 128].rearrange("t p d -> p t d"),
                                                  obs, obsb, reads=[obsb])
                                    after.append(fin)
                                groups.append({"qk": qk, "n": 512, "pv": pv, "after": after})
                    run_groups(groups)
                P.barrier()

            if "z" in phases:
                zst_rot = Rot([(arena[:, k * 512:(k + 1) * 512], P.buf("zst")) for k in range(3)])
                for blk in range(9):
                    c0 = Z_OFF + blk * 512
                    tot = load_weights(l, [(c0, 512)])
                    cast_weights(tot)
                    for tile in range(NT):
                        pj, pjb = pj_rot.next()
                        project(0, tile, 0, 512, pj, pjb)
                        zs, zsb = zst_rot.next()
                        fn = AF.Silu if blk < 3 else AF.Sigmoid
                        P.op("act", lambda e, zs=zs, pj=pj, fn=fn: e.activation(out=zs, in_=pj[:, 0:512], func=fn), reads=[pjb], writes=[zsb])
                        if blk < 3:
                            P.dma(zs_d[tile, :, blk * 512:(blk + 1) * 512], zs, zsb, reads=[zsb])
                        else:
                            P.dma(gs_d[tile, :, (blk - 3) * 512:(blk - 2) * 512], zs, zsb, reads=[zsb])
                P.barrier()

            if "f" in phases:
                wbr = hTflat[:, 0:12288].rearrange("p (n c d) -> p n c d", n=3, c=4)
                wo = hTflat[:, 12288:20480].rearrange("p (c d) -> p c d", c=8)
                WF = P.buf("wfinal")
                for n in range(3):
                    for hf in range(2):
                        P.dma(wst[:, 0:4, :], wbr_d[l, n * 512:(n + 1) * 512, hf * 512:(hf + 1) * 512].rearrange("(c p) d -> p c d", p=128),
                              WST, writes=[WST])
                        P.op("pool", lambda e, n=n, hf=hf: e.tensor_copy(out=wbr[:, n, :, hf * 512:(hf + 1) * 512], in_=wst[:, 0:4, :]),
                             reads=[WST, HT], writes=[WF])
                for hf in range(2):
                    P.dma(wst[:, :, :], wout_d[l, :, hf * 512:(hf + 1) * 512].rearrange("(c p) d -> p c d", p=128), WST, writes=[WST])
                    P.op("pool", lambda e, hf=hf: e.tensor_copy(out=wo[:, :, hf * 512:(hf + 1) * 512], in_=wst[:, :, :]),
                         reads=[WST, HT], writes=[WF])
                xt_rot = Rot([(arenaF[:, i * 1024:(i + 1) * 1024], P.buf("fx")) for i in range(2)])
                oat = arenaF[:, 2048:2048 + 1560].rearrange("p (g h d) -> p g h d", g=3, h=8)
                OAT = P.buf("oat")
                msum = arenaF[:, 3608:3608 + 1024]
                MS = P.buf("msum")
                outt = arenaF[:, 4632:4632 + 1024]
                OUTT = P.buf("outt")
                den = sb("den", [128, 8], F32) if l == 0 else den
                DEN = P.buf("den")
                obt_rot = Rot([(arena[:, k * 1024:(k + 1) * 1024], P.buf("obt")) for k in range(2)])
                zt_rot = Rot([(arena[:, 2048 + k * 1536:2048 + (k + 1) * 1536], P.buf("zt")) for k in range(2)])
                gt_rot = Rot([(arena[:, 5120 + k * 3072:5120 + (k + 1) * 3072], P.buf("gt")) for k in range(2)])
                ub = arena[:, 11264:12800]
                UB = P.buf("ub")
                uT = arena[:, 12800:14336].rearrange("p (c t) -> p c t", c=12)
                UT = P.buf("uT")
                m16 = arena[:, 14336:15360]
                M16 = P.buf("m16")
                mT = arena[:, 15360:16384].rearrange("p (c t) -> p c t", c=8)
                MT = P.buf("mT")
                gy = arenaF[:, 5656:5656 + 512] if False else None
                uT_rot = Rot([(arena[:, 12800:14336].rearrange("p (c t) -> p c t", c=12), P.buf("uT")),
                                  (arena[:, 16384:17920].rearrange("p (c t) -> p c t", c=12), P.buf("uT"))])
                fst = {}

                def f_stage1(tt):
                        xt, xtb = xt_rot.next()
                        uT, UT = uT_rot.next()
                        obt, obtb = obt_rot.next()
                        zt, ztb = zt_rot.next()
                        gtile, gtb = gt_rot.next()
                        P.dma(xt, xsrc[tt * 128:(tt + 1) * 128, :], xtb, reads=[XS], writes=[xtb])
                        P.dma(oat.rearrange("p g h d -> p (g h d)"), oa_d[tt * 128:(tt + 1) * 128].rearrange("t g h d -> t (g h d)"), OAT, writes=[OAT])
                        P.dma(obt, obc_d[tt, :, :], obtb, writes=[obtb])
                        P.dma(zt, zs_d[tt, :, :], ztb, writes=[ztb])
                        P.dma(gtile, gs_d[tt, :, :], gtb, writes=[gtb])
                        P.op("dve", lambda e: e.tensor_tensor(out=oat[:, 0], in0=oat[:, 0], in1=oat[:, 1], op=ALU.add), reads=[OAT], writes=[OAT])
                        P.op("dve", lambda e: e.tensor_tensor(out=oat[:, 0], in0=oat[:, 0], in1=oat[:, 2], op=ALU.add), reads=[OAT], writes=[OAT])
                        P.op("dve", lambda e: e.reciprocal(out=den[:], in_=oat[:, 0, :, 64]), reads=[OAT], writes=[DEN])
                        P.op("dve", lambda e: e.tensor_tensor(out=oat[:, 1, :, 0:64], in0=oat[:, 0, :, 0:64],
                                                              in1=den[:].unsqueeze(2).to_broadcast([128, 8, 64]), op=ALU.mult),
                             reads=[OAT, DEN], writes=[OAT])
                        P.op("dve", lambda e, zt=zt: e.tensor_tensor(out=ub[:, 0:512].rearrange("p (h d) -> p h d", h=8), in0=oat[:, 1, :, 0:64],
                                                                     in1=zt[:, 0:512].rearrange("p (h d) -> p h d", h=8), op=ALU.mult),
                             reads=[OAT, ztb], writes=[UB])
                        P.op("pool", lambda e, zt=zt, obt=obt: e.tensor_tensor(out=ub[:, 512:1536], in0=obt, in1=zt[:, 512:1536], op=ALU.mult),
                             reads=[obtb, ztb], writes=[UB])
                        for half in range(2):
                            nchunk = 8 if half == 0 else 4
                            for cc in range(nchunk):
                                c = half * 8 + cc
                                P.op("pe", lambda e, c=c, cc=cc: e.transpose(out=TPB[:, cc, :], in_=ub[:, c * 128:(c + 1) * 128], identity=ident[:]),
                                     reads=[UB, CONST], writes=TB if cc in (0, nchunk - 1) else [], inc=(cc == nchunk - 1))
                            P.op("act", lambda e, half=half, nchunk=nchunk: e.activation(out=uT[:, half * 8:half * 8 + nchunk, :], in_=TPB[:, 0:nchunk, :],
                                                                                         func=AF.Copy), reads=TB, writes=[UT])
                        fst[tt] = (xt, xtb, gtile, gtb, uT, UT)

                def f_stage2(tt):
                        xt, xtb, gtile, gtb, uT, UT = fst.pop(tt)
                        for n in range(3):
                            for hf in range(2):
                                yp, ypb = st_rot.next()
                                for cc in range(4):
                                    P.op("pe", lambda e, yp=yp, n=n, cc=cc, hf=hf: e.matmul(yp[:, 0:512], lhsT=uT[:, n * 4 + cc, :],
                                                                                            rhs=wbr[:, n, cc, hf * 512:(hf + 1) * 512],
                                                                                            start=(cc == 0), stop=(cc == 3)),
                                         reads=[UT, WF], writes=[ypb] if cc in (0, 3) else [], inc=(cc == 3))
                                msl = msum[:, hf * 512:(hf + 1) * 512]
                                gsl = gtile[:, n * 1024 + hf * 512:n * 1024 + (hf + 1) * 512]
                                if n == 0:
                                    P.op("dve", lambda e, yp=yp, msl=msl, gsl=gsl: e.tensor_tensor(out=msl, in0=yp[:, 0:512], in1=gsl, op=ALU.mult),
                                         reads=[ypb, gtb], writes=[MS])
                                else:
                                    tmp, tmpb = qk_rot.next()
                                    P.op("dve", lambda e, yp=yp, tmp=tmp, gsl=gsl: e.tensor_tensor(out=tmp[:, 0:512], in0=yp[:, 0:512], in1=gsl, op=ALU.mult),
                                         reads=[ypb, gtb], writes=[tmpb])
                                    P.op("pool", lambda e, tmp=tmp, msl=msl: e.tensor_tensor(out=msl, in0=msl, in1=tmp[:, 0:512], op=ALU.add),
                                         reads=[tmpb, MS], writes=[MS])
                        P.op("pool", lambda e: e.tensor_copy(out=m16, in_=msum), reads=[MS], writes=[M16])
                        for cc in range(8):
                            P.op("pe", lambda e, cc=cc: e.transpose(out=TPB[:, cc, :], in_=m16[:, cc * 128:(cc + 1) * 128], identity=ident[:]),
                                 reads=[M16, CONST], writes=TB if cc in (0, 7) else [], inc=(cc == 7))
                        P.op("act", lambda e: e.activation(out=mT, in_=TPB[:, :, :], func=AF.Copy), reads=TB, writes=[MT])
                        for hf in range(2):
                            yp, ypb = st_rot.next()
                            for cc in range(8):
                                P.op("pe", lambda e, yp=yp, cc=cc, hf=hf: e.matmul(yp[:, 0:512], lhsT=mT[:, cc, :], rhs=wo[:, cc, hf * 512:(hf + 1) * 512],
                                                                                  start=(cc == 0), stop=(cc == 7)),
                                     reads=[MT, WF], writes=[ypb] if cc in (0, 7) else [], inc=(cc == 7))
                            P.op("dve", lambda e, yp=yp, hf=hf, xt=xt: e.tensor_tensor(out=outt[:, hf * 512:(hf + 1) * 512], in0=yp[:, 0:512],
                                                                                       in1=xt[:, hf * 512:(hf + 1) * 512], op=ALU.add),
                                 reads=[ypb, xtb], writes=[OUTT])
                        XD = XS if xdst is xsrc else P.buf("xd")
                        P.dma(xdst[tt * 128:(tt + 1) * 128, :], outt, OUTT, reads=[OUTT], writes=[])

                f_stage1(0)
                for tt in range(NT):
                    if tt + 1 < NT:
                        f_stage1(tt + 1)
                    f_stage2(tt)
                P.barrier()

        P.finish("sp")
        P.emit()
    print("bass program: %d instrs, %d sems" % (P.ninstr, len(P.sems)))
    return nc


def _perm_positions(pos_row):
    out = np.empty((128, 3 * NT), np.int32)
    for g in range(3):
        for tile in range(NT):
            out[:, g * NT + tile] = pos_row[tok_slice(g, tile)]
    return out


_CACHE = {}


def _get_prog(L):
    if L not in _CACHE:
        _CACHE[L] = build_program(L)
    return _CACHE[L]


FUSED = True


def kernel(x, positions, norm_g, w_in, qk_g, w_br, w_out):
    x = np.ascontiguousarray(np.asarray(x, np.float32))
    positions = np.asarray(positions, np.int32)
    norm_g = np.ascontiguousarray(np.asarray(norm_g, np.float32))
    w_in = np.ascontiguousarray(np.asarray(w_in, np.float32))
    qk_g2 = np.ascontiguousarray(np.asarray(qk_g, np.float32).reshape(2, 384))
    w_br2 = np.ascontiguousarray(np.asarray(w_br, np.float32).reshape(2, 1536, D))
    w_out = np.ascontiguousarray(np.asarray(w_out, np.float32))
    ncores = x.shape[0]
    posp = [_perm_positions(positions[b]) for b in range(ncores)]
    if FUSED:
        nc = _get_prog(2)
        in_maps = [{"x": x[b], "posp": posp[b], "norm_g": norm_g, "w_in": w_in, "qk_g": qk_g2, "w_br": w_br2, "w_out": w_out}
                   for b in range(ncores)]
        res = run_bass_kernel_spmd(nc, in_maps, core_ids=list(range(ncores)))
        return np.stack([np.asarray(r["out"], np.float32) for r in res.results], axis=0)
    nc = _get_prog(1)
    cur = [x[b] for b in range(ncores)]
    for l in range(2):
        in_maps = [{"x": cur[b], "posp": posp[b], "norm_g": norm_g[l:l + 1], "w_in": w_in[l:l + 1], "qk_g": qk_g2[l:l + 1],
                    "w_br": w_br2[l:l + 1], "w_out": w_out[l:l + 1]} for b in range(ncores)]
        res = run_bass_kernel_spmd(nc, in_maps, core_ids=list(range(ncores)))
        cur = [np.ascontiguousarray(np.asarray(r["out"], np.float32)) for r in res.results]
    return np.stack(cur, axis=0)
```

```python
import math
from contextlib import ExitStack

import numpy as np
import concourse.bass as bass
import concourse.mybir as mybir
from concourse.bass_utils import run_bass_kernel_spmd

F32 = mybir.dt.float32
BF16 = mybir.dt.bfloat16
I32 = mybir.dt.int32
ALU = mybir.AluOpType
AF = mybir.ActivationFunctionType
AX = mybir.AxisListType

S = 4096
NT = 32
D = 1024
NIN = 12872
HD = 64
DILS = (1, 4, 16)
A_OFF = 0
B_OFF = 4608
IQ_OFF = 6144
IK_OFF = 6656
IW_OFF = 6720
C_OFF = 6728
Z_OFF = 8264
G_OFF = 9800
EPS = 1e-6
BIG = 30000.0
TOPK = 256
NBIS = 16
ATT_SCALE = 0.125

ENGS = ("pe", "act", "dve", "pool", "sp")


class Buf:
    __slots__ = ("name", "lw", "rd", "dsem")

    def __init__(self, name):
        self.name = name
        self.lw = None
        self.rd = []
        self.dsem = None


class Tok:
    __slots__ = ("key", "count")

    def __init__(self, key, count=None):
        self.key = key
        self.count = count


class Prog:
    def __init__(self, nc, stack):
        self.nc = nc
        self.stack = stack
        self.ops = {e: [] for e in ENGS}
        self.sems = {}
        self.cnt = {}
        self.known = {e: {} for e in ENGS}
        self.pending = {e: [] for e in ENGS}
        for e in ENGS:
            if e != "sp":
                self._newsem(e)
        self.nbuf = 0
        self.ninstr = 0

    def _newsem(self, key):
        self.sems[key] = self.stack.enter_context(self.nc.semaphore("s%d" % len(self.sems)))
        self.cnt[key] = 0

    def buf(self, name=None):
        self.nbuf += 1
        return Buf("%s_%d" % (name or "b", self.nbuf))

    def bufs(self, n, name="b"):
        return [self.buf(name) for _ in range(n)]

    def _deps(self, reads, writes):
        deps = []
        for b in reads:
            if b.lw is not None:
                deps.append(b.lw)
        for b in writes:
            if b.lw is not None:
                deps.append(b.lw)
            deps.extend(b.rd)
        return deps

    def _emit_waits(self, eng, deps):
        need = {}
        for t in deps:
            if t.key == eng == "pe":
                continue
            if t.count is None:
                raise RuntimeError("dependency on instruction without inc (%s -> %s)" % (t.key, eng))
            if need.get(t.key, 0) < t.count:
                need[t.key] = t.count
        kn = self.known[eng]
        for key, c in need.items():
            if kn.get(key, 0) >= c:
                continue
            kn[key] = c
            sem = self.sems[key]
            self.ops[eng].append(lambda e, sem=sem, c=c: e.wait_ge(sem, c))
            self.ninstr += 1

    def _register(self, tok, reads, writes):
        for b in reads:
            b.rd.append(tok)
        for b in writes:
            b.lw = tok
            b.rd = []

    def op(self, eng, fn, reads=(), writes=(), inc=True):
        self._emit_waits(eng, self._deps(reads, writes))
        tok = Tok(eng)
        self.ninstr += 1
        if inc:
            self.cnt[eng] += 1
            c = self.cnt[eng]
            tok.count = c
            for t in self.pending[eng]:
                t.count = c
            self.pending[eng] = []
            sem = self.sems[eng]
            self.ops[eng].append(lambda e, fn=fn, sem=sem: fn(e).then_inc(sem, 1))
        else:
            self.pending[eng].append(tok)
            self.ops[eng].append(lambda e, fn=fn: fn(e))
        self._register(tok, reads, writes)
        return tok

    def dma(self, out_ap, in_ap, sb, reads=(), writes=(), eng="sp"):
        if sb.dsem is None:
            sb.dsem = "d%d" % len(self.sems)
            self._newsem(sb.dsem)
        key = sb.dsem
        self._emit_waits(eng, self._deps(reads, writes))
        self.cnt[key] += 16
        tok = Tok(key, self.cnt[key])
        sem = self.sems[key]
        self.ninstr += 1
        self.ops[eng].append(
            lambda e, o=out_ap, i=in_ap, sem=sem: e.dma_start(out=o, in_=i).then_inc(sem, 16))
        self._register(tok, reads, writes)
        return tok

    def barrier(self):
        snap = dict(self.cnt)
        for e in ENGS:
            if self.pending[e]:
                raise RuntimeError("barrier with pending instrs on " + e)
        for e in ENGS:
            kn = self.known[e]
            for key, c in snap.items():
                if c > 0 and kn.get(key, 0) < c:
                    kn[key] = c
                    sem = self.sems[key]
                    self.ops[e].append(lambda eng, sem=sem, c=c: eng.wait_ge(sem, c))

    def finish(self, eng="sp"):
        kn = self.known[eng]
        for key, c in self.cnt.items():
            if c > 0 and kn.get(key, 0) < c:
                kn[key] = c
                sem = self.sems[key]
                self.ops[eng].append(lambda e, sem=sem, c=c: e.wait_ge(sem, c))

    def emit(self):
        with self.nc.Block() as block:
            @block.tensor
            def _(e):
                for f in self.ops["pe"]:
                    f(e)

            @block.scalar
            def _(e):
                for f in self.ops["act"]:
                    f(e)

            @block.vector
            def _(e):
                for f in self.ops["dve"]:
                    f(e)

            @block.gpsimd
            def _(e):
                for f in self.ops["pool"]:
                    f(e)

            @block.sync
            def _(e):
                for f in self.ops["sp"]:
                    f(e)


class Rot:
    def __init__(self, items):
        self.items = items
        self.i = 0

    def next(self):
        it = self.items[self.i % len(self.items)]
        self.i += 1
        return it


def tok_slice(g, tile):
    d = DILS[g]
    nb = NT // d
    r, i = tile // nb, tile % nb
    start = 128 * i * d + r
    return slice(start, start + 127 * d + 1, d)


ALL_PHASES = ("p0", "a", "idx", "b", "c", "z", "f")


def build_program(L, dbg=False, phases=ALL_PHASES):
    nc = bass.Bass("TRN2", target_bir_lowering=False)
    okind = "ExternalOutput" if dbg else "Internal"
    x_in = nc.dram_tensor("x", [S, D], F32, kind="ExternalInput").ap()
    posp = nc.dram_tensor("posp", [128, 3 * NT], I32, kind="ExternalInput").ap()
    normg_d = nc.dram_tensor("norm_g", [L, D], F32, kind="ExternalInput").ap()
    win_d = nc.dram_tensor("w_in", [L, D, NIN], F32, kind="ExternalInput").ap()
    qkg_d = nc.dram_tensor("qk_g", [L, 384], F32, kind="ExternalInput").ap()
    wbr_d = nc.dram_tensor("w_br", [L, 1536, D], F32, kind="ExternalInput").ap()
    wout_d = nc.dram_tensor("w_out", [L, D, D], F32, kind="ExternalInput").ap()
    out_d = nc.dram_tensor("out", [S, D], F32, kind="ExternalOutput").ap()
    oa_d = nc.dram_tensor("oa_aug", [S, 3, 8, 65], F32, kind=okind).ap()
    obc_d = nc.dram_tensor("o_bc", [NT, 128, 1024], BF16, kind=okind).ap()
    zs_d = nc.dram_tensor("zs", [NT, 128, 1536], BF16, kind=okind).ap()
    gs_d = nc.dram_tensor("gs", [NT, 128, 3072], BF16, kind=okind).ap()
    mk_d = nc.dram_tensor("maskT", [8, NT, 128, 512], BF16, kind=okind).ap()
    x1_d = nc.dram_tensor("x1s", [S, D], F32, kind="Internal").ap() if L > 1 else None

    with ExitStack() as st:
        P = Prog(nc, st)

        def sb(name, shape, dt):
            return st.enter_context(nc.sbuf_tensor(name, shape, dt))

        def ps(name, shape, dt):
            return st.enter_context(nc.psum_tensor(name, shape, dt))

        PSB = [(ps("psb%d" % i, [128, 512], F32), P.buf("psb")) for i in range(5)]
        OBK = [(ps("obk%d" % i, [128, 512], F32), P.buf("obk")) for i in range(2)]
        TPB = ps("tpb", [128, 8, 128], BF16)
        TBK = P.buf("tpb")
        TB = [TBK]
        pj_rot = Rot([(t[:], b) for t, b in PSB[0:2]])
        st_rot = Rot([(t[:], b) for t, b in PSB[2:5]])
        tp_rot = Rot([(TPB[:, 2 * i:2 * i + 2, :], TBK) for i in range(4)])
        o_rot = Rot([(t[:, 0:128], b) for t, b in OBK])
        o4_rot = Rot([(t[:].rearrange("p (u d) -> p u d", u=4), b) for t, b in (OBK + PSB[0:2])])

        ident = sb("ident", [128, 128], BF16)
        identB = sb("identB", [128, 128], BF16)
        band = sb("band", [128, 256], BF16)
        triq = sb("triq", [128, 128], F32)
        eb = sb("eb", [16, 16, 128], BF16)
        neghalf = sb("neghalf", [128, 8], F32)
        inv2pi = sb("inv2pi", [128, 8], F32)
        pow2 = sb("pow2", [128, NBIS + 1], F32)
        thrneg = sb("thrneg", [128, 1], F32)
        negbig16 = sb("negbig16", [128, 16], F32)
        cosT = sb("cosT", [128, 3 * NT, 8], F32)
        sinT = sb("sinT", [128, 3 * NT, 8], F32)
        CONST = P.buf("const")

        def pool_c(fn):
            P.op("pool", fn, reads=[CONST], writes=[CONST])

        pool_c(lambda e: e.memset(ident[:], 1.0))
        pool_c(lambda e: e.affine_select(out=ident[:], in_=ident[:], pattern=[[-1, 128]], compare_op=ALU.is_equal,
                                         fill=0.0, base=0, channel_multiplier=1))
        pool_c(lambda e: e.memset(identB[:], BIG))
        pool_c(lambda e: e.affine_select(out=identB[:], in_=identB[:], pattern=[[-1, 128]], compare_op=ALU.is_equal,
                                         fill=0.0, base=0, channel_multiplier=1))
        pool_c(lambda e: e.memset(band[:], 0.0))
        pool_c(lambda e: e.affine_select(out=band[:, 0:128], in_=band[:, 0:128], pattern=[[1, 128]], compare_op=ALU.is_ge,
                                         fill=-1.0, base=0, channel_multiplier=-1))
        pool_c(lambda e: e.affine_select(out=band[:, 128:256], in_=band[:, 128:256], pattern=[[-1, 128]], compare_op=ALU.is_ge,
                                         fill=-1.0, base=0, channel_multiplier=1))
        pool_c(lambda e: e.memset(triq[:], 0.0))
        pool_c(lambda e: e.affine_select(out=triq[:], in_=triq[:], pattern=[[-1, 128]], compare_op=ALU.is_ge,
                                         fill=-1e30, base=0, channel_multiplier=1))
        pool_c(lambda e: e.memset(eb[:], BIG))
        pool_c(lambda e: e.affine_select(out=eb[:], in_=eb[:], pattern=[[-1, 16], [0, 128]], compare_op=ALU.is_equal,
                                         fill=0.0, base=0, channel_multiplier=1))
        pool_c(lambda e: e.memset(dstrip[:], 0.0))
        pool_c(lambda e: e.memset(dstrip[:, 0:384], -1.0))
        pool_c(lambda e: e.tensor_copy(out=dstrip[:, 384:512], in_=band[:, 0:128]))
        pool_c(lambda e: e.memset(neghalf[:], -0.5))
        pool_c(lambda e: e.memset(thrneg[:], -1e29))
        pool_c(lambda e: e.memset(negbig16[:], -1e30))
        for i in range(8):
            v = (500000.0 ** (-(2.0 * i) / 16.0)) / (2 * math.pi)
            pool_c(lambda e, i=i, v=v: e.memset(inv2pi[:, i:i + 1], v))
        for k in range(NBIS + 1):
            pool_c(lambda e, k=k: e.memset(pow2[:, k:k + 1], 2.0 ** (-k)))

        hT = sb("hT", [128, 8, S], BF16)
        HT = P.buf("hT")
        hTflat = hT[:].rearrange("p c t -> p (c t)")
        qkg = sb("qkg", [128, 384], F32)
        GB = P.buf("gparams")
        WCOLS = 512
        wst = sb("wst", [128, 8, WCOLS], F32)
        WST = P.buf("wst")
        wbf = sb("wbf", [128, 8, WCOLS], BF16)
        WBF = P.buf("wbf")
        NB16 = 28800
        NF32 = 6144
        arena = sb("arena", [128, NB16], BF16)
        arenaF = sb("arenaF", [128, NF32], F32)
        junk8 = sb("junk8", [128, S], mybir.dt.uint8)
        JK8 = P.buf("junk8")

        def mk_rot(name, n, shape, dt):
            return Rot([(sb("%s%d" % (name, i), shape, dt)[:], P.buf(name)) for i in range(n)])

        qk_rot = mk_rot("qksb", 2, [128, 512], F32)
        sq_rot = mk_rot("sqsb", 1, [128, 512], F32)
        ss_rot = mk_rot("ss", 4, [128, 8], F32)
        rs_rot = mk_rot("rs", 4, [128, 8], F32)
        qb_rot = mk_rot("qb", 2, [128, 512], BF16)
        ra_rot = mk_rot("ra", 2, [128, 8, 2, 8], F32)
        rb_rot = mk_rot("rb", 2, [128, 8, 2, 8], F32)
        pt_rot = mk_rot("pt", 3, [128, 512], BF16)
        oas_rot = mk_rot("oas", 3, [128, 2, 65], F32)
        rc4_rot = mk_rot("rc4", 3, [128, 4], F32)
        qkh_rot = Rot([(t[:, h * 256:(h + 1) * 256], P.buf("qkh")) for t, _ in qk_rot.items for h in range(2)])
        sqh_rot = Rot([(t[:, h * 256:(h + 1) * 256], P.buf("sqh")) for t, _ in sq_rot.items for h in range(2)])
        qbh_rot = Rot([(t[:, h * 256:(h + 1) * 256], P.buf("qbh")) for t, _ in qb_rot.items for h in range(2)])
        pjp_rot = Rot([(t[:], b) for t, b in PSB[0:4]])
        obs4_rot = mk_rot("obs4", 2, [128, 4, 128], BF16)
        dstrip = sb("dstrip", [128, 7 * 128], BF16)

        posi = sb("posi", [128, 3 * NT], I32)
        posf = sb("posf", [128, 3 * NT], F32)
        ru = arenaF[:, 0:768].rearrange("p (t i) -> p t i", i=8)
        rf = arenaF[:, 768:1536].rearrange("p (t i) -> p t i", i=8)
        rk_ap = arenaF[:, 1536:2304].bitcast(I32).rearrange("p (t i) -> p t i", i=8)
        RB = P.buf("rope")
        P.dma(posi[:], posp[:, :], RB, writes=[RB])
        P.op("dve", lambda e: e.tensor_copy(out=posf[:], in_=posi[:]), reads=[RB], writes=[RB])
        for tab, shift in ((sinT, 0.0), (cosT, 0.25)):
            P.op("dve", lambda e: e.tensor_tensor(out=ru, in0=posf[:].unsqueeze(2).to_broadcast([128, 3 * NT, 8]),
                                                  in1=inv2pi[:].unsqueeze(1).to_broadcast([128, 3 * NT, 8]), op=ALU.mult),
                 reads=[RB, CONST], writes=[RB])
            if shift:
                P.op("dve", lambda e, s_=shift: e.tensor_scalar(out=ru, in0=ru, scalar1=s_, scalar2=None, op0=ALU.add),
                     reads=[RB], writes=[RB])
            P.op("dve", lambda e: e.tensor_copy(out=rk_ap, in_=ru), reads=[RB], writes=[RB])
            P.op("dve", lambda e: e.tensor_copy(out=rf, in_=rk_ap), reads=[RB], writes=[RB])
            P.op("dve", lambda e: e.tensor_tensor(out=ru, in0=ru, in1=rf, op=ALU.subtract), reads=[RB], writes=[RB])
            P.op("dve", lambda e: e.scalar_tensor_tensor(out=rf, in0=ru, scalar=0.5, in1=ru, op0=ALU.is_gt, op1=ALU.subtract),
                 reads=[RB], writes=[RB])
            P.op("act", lambda e, tab=tab: e.activation(out=tab[:], in_=rf, func=AF.Sin, scale=-2 * math.pi),
                 reads=[RB], writes=[RB, CONST])
        P.barrier()

        wq = {"list": [], "pos": 0, "loaded": -1}

        def _issue_load(idx):
            l_, segs = wq["list"][idx]
            off = 0
            for (c0, n) in segs:
                P.dma(wst[:, :, off:off + n], win_d[l_, :, c0:c0 + n].rearrange("(c p) n -> p c n", p=128), WST,
                      writes=[WST])
                off += n
            wq["loaded"] = idx

        def load_weights(l, segs):
            idx = wq["pos"]
            assert wq["list"][idx] == (l, segs), (wq["list"][idx], (l, segs))
            wq["pos"] += 1
            if wq["loaded"] < idx:
                _issue_load(idx)
            tot = sum(n for _, n in segs)
            P.op("act", lambda e, tot=tot: e.activation(out=wbf[:, :, 0:tot], in_=wst[:, :, 0:tot], func=AF.Copy),
                 reads=[WST], writes=[WBF])
            if idx + 1 < len(wq["list"]):
                _issue_load(idx + 1)
            return tot

        def cast_weights(tot):
            return None

        def project(g, tile, c0, n, pj, pjb):
            sl = tok_slice(g, tile)
            for c in range(8):
                P.op("pe", lambda e, c=c: e.matmul(pj[:, 0:n], lhsT=hT[:, c, sl], rhs=wbf[:, c, c0:c0 + n],
                                                  start=(c == 0), stop=(c == 7)),
                     reads=[HT, WBF], writes=[pjb] if c in (0, 7) else [], inc=(c == 7))

        def rope(src, srcb, dst, dstb, nu, ctile, eng="pool", eng2=None):
            eng2 = eng2 or eng
            ra, rab = ra_rot.next()
            rb, rbb = rb_rot.next()
            x12 = src.rearrange("p (u d) -> p u d", u=nu)[:, :, 0:16].rearrange("p u (h d) -> p u h d", h=2)
            cosb = cosT[:, ctile:ctile + 1, :].unsqueeze(1).to_broadcast([128, nu, 2, 8])
            sinb = sinT[:, ctile:ctile + 1, :].unsqueeze(1).to_broadcast([128, nu, 2, 8])
            P.op(eng, lambda e: e.tensor_tensor(out=ra[:, 0:nu], in0=x12, in1=cosb, op=ALU.mult),
                 reads=[srcb, CONST], writes=[rab])
            P.op(eng, lambda e: e.tensor_tensor(out=rb[:, 0:nu], in0=x12, in1=sinb, op=ALU.mult),
                 reads=[srcb, CONST], writes=[rbb])
            d3 = dst.rearrange("p (u d) -> p u d", u=nu)
            P.op(eng2, lambda e: e.tensor_tensor(out=d3[:, :, 0:8], in0=ra[:, 0:nu, 0, :], in1=rb[:, 0:nu, 1, :], op=ALU.subtract),
                 reads=[rab, rbb], writes=[dstb])
            P.op(eng2, lambda e: e.tensor_tensor(out=d3[:, :, 8:16], in0=ra[:, 0:nu, 1, :], in1=rb[:, 0:nu, 0, :], op=ALU.add),
                 reads=[rab, rbb], writes=[dstb])

        def run_groups(groups):
            prev = None
            for gi in range(len(groups) + 1):
                cur = None
                if gi < len(groups):
                    gd = groups[gi]
                    for f in gd.get("before", ()):
                        f()
                    stt, stb = st_rot.next()
                    ptt, ptb = pt_rot.next()
                    nq = len(gd["qk"])
                    for qi, (fn, rds) in enumerate(gd["qk"]):
                        P.op("pe", lambda e, fn=fn, stt=stt: fn(e, stt), reads=rds,
                             writes=[stb] if qi in (0, nq - 1) else [], inc=(qi == nq - 1))
                    n = gd["n"]
                    n0 = gd.get("n0", 0)
                    P.op("act", lambda e, stt=stt, ptt=ptt, n=n, n0=n0: e.activation(out=ptt[:, n0:n], in_=stt[:, n0:n], func=AF.Exp,
                                                                                      scale=ATT_SCALE),
                         reads=[stb], writes=[ptb])
                    cur = (gd, ptt, ptb)
                if prev is not None:
                    gd, ptt, ptb = prev
                    for (oz, ozb) in gd.get("pre", ()):
                        P.op("dve", lambda e, oz=oz: e.memset(oz[:, 0:65], 0.0), writes=[ozb])
                    for (fn, rds, ob) in gd["pv"]:
                        P.op("pe", lambda e, fn=fn, ptt=ptt: fn(e, ptt), reads=[ptb] + rds, writes=[ob], inc=True)
                    for f in gd["after"]:
                        f()
                prev = cur

        QT = [arena[:, 0:4096], arena[:, 4096:8192]]
        KT = [arena[:, 8192:12288], arena[:, 12288:16384]]
        QKB = P.buf("qkt")
        VA = arena[:, 16384:16384 + NT * 130].rearrange("p (t h d) -> p t h d", t=NT, h=2)
        VB = P.buf("v")
        A_END = 16384 + NT * 130

        EINIT = P.buf("einit")

        def init_qk_tiles(moba):
            P.op("pool", lambda e: e.memset(KT[0][64:128, :], 0.0), reads=[QKB], writes=[QKB])
            P.op("pool", lambda e: e.memset(KT[1][0:64, :], 0.0), reads=[QKB], writes=[QKB])
            if moba:
                P.op("pool", lambda e: e.memset(QT[0][64:128, :], 0.0), reads=[QKB], writes=[QKB])
                P.op("pool", lambda e: e.memset(QT[1][0:64, :], 0.0), reads=[QKB], writes=[QKB])
                P.op("pool", lambda e: e.memset(KT[1][0:16, :], BIG), reads=[QKB], writes=[QKB])
                P.op("pool", lambda e: e.affine_select(out=KT[1][0:16, :], in_=KT[1][0:16, :], pattern=[[1, S]], compare_op=ALU.is_ge,
                                                       fill=0.0, base=0, channel_multiplier=-256), reads=[QKB], writes=[QKB])
                P.op("pool", lambda e: e.affine_select(out=KT[1][0:16, :], in_=KT[1][0:16, :], pattern=[[-1, S]], compare_op=ALU.is_ge,
                                                       fill=0.0, base=255, channel_multiplier=256), reads=[QKB], writes=[QKB])
                P.dma(KT[0][64:80, :], KT[1][0:16, :], EINIT, reads=[QKB], writes=[QKB])

        def prep_pair(l, g, q0, k0, v0, mixer, moba=False):
            tot = load_weights(l, [(q0, 128), (k0, 128), (v0, 128)])
            cast_weights(tot)
            P.op("pool", lambda e: e.memset(VA[:, :, :, 64:65], 1.0), reads=[VB], writes=[VB])
            LA = 3
            pjs = {}

            def stage1(t):
                pjs[t] = pjp_rot.next()
                project(g, t, 0, 384, pjs[t][0], pjs[t][1])
            st2 = {}

            stc = {}

            def stage2c(tile):
                pj, pjb = pjs.pop(tile)
                qs, qsb = qkh_rot.next()
                stc[tile] = (qs, qsb)
                P.op("act", lambda e, qs=qs, pj=pj: e.activation(out=qs[:, 0:256], in_=pj[:, 0:256], func=AF.Copy),
                     reads=[pjb], writes=[qsb])
                P.op("act", lambda e, pj=pj, tile=tile: e.activation(out=VA[:, tile, :, 0:64],
                                                                     in_=pj[:, 256:384].rearrange("p (h d) -> p h d", h=2),
                                                                     func=AF.Copy),
                     reads=[pjb], writes=[VB])

            def stage2a(tile):
                qs, qsb = stc.pop(tile)
                sq, sqb = sqh_rot.next()
                ss, ssb = ss_rot.next()
                rs, rsb = rs_rot.next()
                st2[tile] = (qs, qsb, rs, rsb)
                P.op("dve", lambda e, qs=qs, sq=sq: e.tensor_tensor(out=sq[:, 0:256], in0=qs[:, 0:256], in1=qs[:, 0:256], op=ALU.mult),
                     reads=[qsb], writes=[sqb])
                P.op("dve", lambda e, sq=sq, ss=ss: e.tensor_reduce(out=ss[:, 0:4], in_=sq[:, 0:256].rearrange("p (u d) -> p u d", u=4),
                                                                    axis=AX.X, op=ALU.add),
                     reads=[sqb], writes=[ssb])
                P.op("pool", lambda e, ss=ss: e.tensor_scalar(out=ss[:, 0:4], in0=ss[:, 0:4], scalar1=1.0 / HD, scalar2=EPS,
                                                              op0=ALU.mult, op1=ALU.add), reads=[ssb], writes=[ssb])
                P.op("pool", lambda e, ss=ss, rs=rs: e.tensor_tensor(out=rs[:, 0:4], in0=ss[:, 0:4], in1=neghalf[:, 0:4], op=ALU.pow),
                     reads=[ssb, CONST], writes=[rsb])

            def stage2b(tile):
                qs, qsb, rs, rsb = st2.pop(tile)
                qb, qbb = qbh_rot.next()
                qs3 = qs[:, 0:256].rearrange("p (u d) -> p u d", u=4)
                P.op("dve", lambda e, qs3=qs3, rs=rs: e.tensor_tensor(out=qs3, in0=qs3,
                                                                      in1=rs[:, 0:4].unsqueeze(2).to_broadcast([128, 4, 64]),
                                                                      op=ALU.mult),
                     reads=[qsb, rsb], writes=[qsb])
                gq = qkg[:, mixer * 128:mixer * 128 + 128].rearrange("p (k d) -> p k d", k=2)
                qs4 = qs[:, 0:256].rearrange("p (k u d) -> p k u d", k=2, u=2)
                qb4 = qb[:, 0:256].rearrange("p (k u d) -> p k u d", k=2, u=2)
                P.op("dve", lambda e, qs4=qs4, qb4=qb4, gq=gq: e.tensor_tensor(out=qb4, in0=qs4,
                                                                               in1=gq.unsqueeze(2).to_broadcast([128, 2, 2, 64]), op=ALU.mult),
                     reads=[qsb, GB], writes=[qbb])
                rope(qb[:, 0:256], qbb, qb[:, 0:256], qbb, 4, g * NT + tile, eng="dve", eng2="pool")
                tp, tpb = tp_rot.next()
                for a in range(2):
                    P.op("pe", lambda e, a=a, tp=tp, qb=qb: e.transpose(out=tp[:, a, :], in_=qb[:, a * 128:(a + 1) * 128], identity=ident[:]),
                         reads=[qbb, CONST], writes=[tpb], inc=(a == 1))
                cs_ = slice(tile * 128, (tile + 1) * 128)
                if moba:
                    P.op("act", lambda e, tp=tp, cs_=cs_: e.activation(out=QT[0][0:64, cs_], in_=tp[0:64, 0, :], func=AF.Copy),
                         reads=[tpb], writes=[QKB])
                    P.op("act", lambda e, tp=tp, cs_=cs_: e.activation(out=QT[1][64:128, cs_], in_=tp[64:128, 0, :], func=AF.Copy),
                         reads=[tpb], writes=[QKB])
                else:
                    P.op("act", lambda e, tp=tp, cs_=cs_: e.activation(out=QT[0][:, cs_], in_=tp[:, 0, :], func=AF.Copy),
                         reads=[tpb], writes=[QKB])
                P.op("act", lambda e, tp=tp, cs_=cs_: e.activation(out=KT[0][0:64, cs_], in_=tp[0:64, 1, :], func=AF.Copy),
                     reads=[tpb], writes=[QKB])
                P.op("act", lambda e, tp=tp, cs_=cs_: e.activation(out=KT[1][64:128, cs_], in_=tp[64:128, 1, :], func=AF.Copy),
                     reads=[tpb], writes=[QKB])

            for t in range(min(LA, NT)):
                stage1(t)
            stage2c(0)
            stage2c(1)
            stage2a(0)
            for tile in range(NT):
                if tile + LA < NT:
                    stage1(tile + LA)
                if tile + 2 < NT:
                    stage2c(tile + 2)
                if tile + 1 < NT:
                    stage2a(tile + 1)
                stage2b(tile)

        for l in range(L):
            xsrc = x_in if l == 0 else x1_d
            XS = P.buf("xsrc")
            xdst = out_d if l == L - 1 else x1_d
            P.dma(qkg[:], qkg_d[l:l + 1, :].partition_broadcast(128), GB, writes=[GB])
            units = []
            if "a" in phases:
                import os
                for g in range(int(os.environ.get("DBG_G0", "0")), int(os.environ.get("DBG_G1", "3"))):
                    for pp in range(int(os.environ.get("DBG_PP", "4"))):
                        hq = (g * 8 + 2 * pp) * HD
                        units.append((l, [(A_OFF + hq, 128), (A_OFF + 1536 + hq, 128), (A_OFF + 3072 + hq, 128)]))
            if "idx" in phases:
                units.append((l, [(IQ_OFF, 512)]))
                units.append((l, [(IK_OFF, 72)]))
            for ph, off in (("b", B_OFF), ("c", C_OFF)):
                if ph in phases:
                    for pp in range(4):
                        hq = 2 * pp * HD
                        units.append((l, [(off + hq, 128), (off + 512 + hq, 128), (off + 1024 + hq, 128)]))
            if "z" in phases:
                for blk in range(9):
                    units.append((l, [(Z_OFF + blk * 512, 512)]))
            wq["list"] = wq["list"][:wq["pos"]] + units

            if "p0" in phases:
                gnorm = arenaF[:, 2048:3072]
                P.dma(gnorm, normg_d[l:l + 1, :].partition_broadcast(128), GB, writes=[GB])
                xt_rot = Rot([(arenaF[:, i * 1024:(i + 1) * 1024], P.buf("xt")) for i in range(2)])
                hb_rot = Rot([(arena[:, i * 1024:(i + 1) * 1024], P.buf("hb")) for i in range(2)])
                junk = arena[:, 2048:3072]
                JK = P.buf("junk")
                p0st = {}

                def p0_a(tt):
                        xt, xtb = xt_rot.next()
                        hb, hbb = hb_rot.next()
                        ss, ssb = ss_rot.next()
                        rs, rsb = rs_rot.next()
                        P.dma(xt, xsrc[tt * 128:(tt + 1) * 128, :], xtb, reads=[XS], writes=[xtb])
                        P.op("act", lambda e, xt=xt, ss=ss: e.activation(out=junk, in_=xt, func=AF.Square, accum_out=ss[:, 0:1]),
                             reads=[xtb], writes=[JK, ssb])
                        P.op("pool", lambda e, ss=ss: e.tensor_scalar(out=ss[:, 0:1], in0=ss[:, 0:1], scalar1=1.0 / D, scalar2=EPS,
                                                                      op0=ALU.mult, op1=ALU.add), reads=[ssb], writes=[ssb])
                        P.op("pool", lambda e, ss=ss, rs=rs: e.tensor_tensor(out=rs[:, 0:1], in0=ss[:, 0:1], in1=neghalf[:, 0:1], op=ALU.pow),
                             reads=[ssb, CONST], writes=[rsb])
                        P.op("dve", lambda e, xt=xt, rs=rs, hb=hb: e.scalar_tensor_tensor(out=hb, in0=xt, scalar=rs[:, 0:1], in1=gnorm,
                                                                                          op0=ALU.mult, op1=ALU.mult),
                             reads=[xtb, rsb, GB], writes=[hbb])
                        p0st[tt] = (hb, hbb)

                def p0_b(tt):
                        hb, hbb = p0st.pop(tt)
                        for cc in range(8):
                            P.op("pe", lambda e, cc=cc, hb=hb: e.transpose(out=TPB[:, cc, :], in_=hb[:, cc * 128:(cc + 1) * 128],
                                                                          identity=ident[:]),
                                 reads=[hbb, CONST], writes=TB if cc in (0, 7) else [], inc=(cc == 7))
                        P.op("act", lambda e, tt=tt: e.activation(out=hT[:, :, tt * 128:(tt + 1) * 128], in_=TPB[:, :, :], func=AF.Copy),
                             reads=TB, writes=[HT])

                p0_a(0)
                for tt in range(NT):
                    if tt + 1 < NT:
                        p0_a(tt + 1)
                    p0_b(tt)
                P.barrier()

            if "a" in phases:
                import os
                init_qk_tiles(False)
                for g in range(int(os.environ.get("DBG_G0", "0")), int(os.environ.get("DBG_G1", "3"))):
                    d = DILS[g]
                    nb = NT // d
                    for pp in range(int(os.environ.get("DBG_PP", "4"))):
                        hq = (g * 8 + 2 * pp) * HD
                        prep_pair(l, g, A_OFF + hq, A_OFF + 1536 + hq, A_OFF + 3072 + hq, 0)
                        groups = []
                        for r in range(d):
                            for j in range(nb):
                                gt = r * nb + j
                                oas_slot = oas_rot.next()
                                for hh in range(2):
                                    base = hh * 64
                                    oj, ojb = o_rot.next()
                                    kts = ([(gt - 1, 128)] if j > 0 else []) + [(gt, 0)]
                                    qk = []
                                    pv = []
                                    for jj, (kt, boff) in enumerate(kts):
                                        def qk1(e, stt, hh=hh, kt=kt, gt=gt, jj=jj):
                                            return e.matmul(stt[:, jj * 128:(jj + 1) * 128], lhsT=KT[hh][:, kt * 128:(kt + 1) * 128],
                                                            rhs=QT[0][:, gt * 128:(gt + 1) * 128], start=True, stop=False)

                                        def qk2(e, stt, jj=jj, boff=boff):
                                            return e.matmul(stt[:, jj * 128:(jj + 1) * 128], lhsT=identB[:], rhs=band[:, boff:boff + 128],
                                                            start=False, stop=True)
                                        qk.append((qk1, [QKB]))
                                        qk.append((qk2, [CONST]))

                                        def pvf(e, ptt, oj=oj, kt=kt, jj=jj, hh=hh, nk=len(kts)):
                                            return e.matmul(oj[:, 0:65], lhsT=ptt[:, jj * 128:(jj + 1) * 128], rhs=VA[:, kt, hh, :],
                                                            start=(jj == 0), stop=(jj == nk - 1))
                                        pv.append((pvf, [VB], ojb))

                                    def evac(oj=oj, ojb=ojb, hh=hh, gt=gt, g=g, pp=pp, oas_slot=oas_slot):
                                        oas, oasb = oas_slot
                                        P.op("dve", lambda e: e.tensor_copy(out=oas[:, hh, :], in_=oj[:, 0:65]), reads=[ojb], writes=[oasb])
                                        if hh == 1:
                                            P.dma(oa_d[tok_slice(g, gt), g, 2 * pp:2 * pp + 2, :], oas, oasb, reads=[oasb])
                                    groups.append({"qk": qk, "n": 128 * len(kts), "pv": pv, "after": [evac]})
                        if os.environ.get("DBG_NOATT") is None:
                            run_groups(groups)
                P.barrier()

            if "idx" in phases:
                QiT = arena[:, 0:16384].rearrange("p (a t) -> p a t", a=4)
                KiT2 = arena[:, 16384:20480]
                mbf = arena[:, 20480:24576]
                mbT = arena[:, 24576:28672].rearrange("p (j q) -> p j q", q=128)
                QIB, KIB, MBB, MBTB = P.buf("qit"), P.buf("kit"), P.buf("mb"), P.buf("mbT")
                acc = arenaF[:, 0:4096]
                ACC = P.buf("acc")
                tmp_rot = Rot([(arenaF[:, 4096 + i * 512:4096 + (i + 1) * 512], P.buf("tmp")) for i in range(2)])
                wsc = arenaF[:, 5120:5376].rearrange("p (t h) -> p t h", h=8)
                WSC = P.buf("wsc")
                bis = sb("bis", [128, 8 + 2 * (NBIS + 1)], F32) if l == 0 else bis
                BIS = P.buf("bis")
                lo, hi, rng_, mid, cntv, dd, sga = (bis[:, i:i + 1] for i in range(7))
                MIDB, CNTB, SGAB, JK8A = P.buf("mid"), P.buf("cnt"), P.buf("sga"), P.buf("jk8a")
                steps2 = bis[:, 8:8 + NBIS + 1]
                steps = bis[:, 8 + NBIS + 1:8 + 2 * (NBIS + 1)]

                for blk in range(2):
                    if blk == 0:
                        tot = load_weights(l, [(IQ_OFF, 512)])
                    else:
                        tot = load_weights(l, [(IK_OFF, 72)])
                    cast_weights(tot)
                    for tile in range(NT):
                        pj, pjb = pj_rot.next()
                        project(0, tile, 0, tot, pj, pjb)
                        qs, qsb = qk_rot.next()
                        qb, qbb = qb_rot.next()
                        if blk == 0:
                            P.op("act", lambda e, qs=qs, pj=pj: e.activation(out=qs[:, 0:512], in_=pj[:, 0:512], func=AF.Copy, scale=0.125),
                                 reads=[pjb], writes=[qsb])
                            P.op("pool", lambda e, qb=qb, qs=qs: e.tensor_copy(out=qb[:, 0:512], in_=qs[:, 0:512]), reads=[qsb], writes=[qbb])
                            rope(qs[:, 0:512], qsb, qb[:, 0:512], qbb, 8, tile)
                            for a in range(4):
                                tp, tpb = tp_rot.next() if a % 2 == 0 else (tp, tpb)
                                P.op("pe", lambda e, a=a, tp=tp, qb=qb: e.transpose(out=tp[:, a % 2, :], in_=qb[:, a * 128:(a + 1) * 128],
                                                                                    identity=ident[:]),
                                     reads=[qbb, CONST], writes=[tpb], inc=(a % 2 == 1))
                                if a % 2 == 1:
                                    P.op("act", lambda e, tp=tp, tile=tile, a=a: e.activation(
                                        out=QiT[:, a - 1:a + 1, tile * 128:(tile + 1) * 128], in_=tp, func=AF.Copy),
                                        reads=[tpb], writes=[QIB])
                        else:
                            P.op("act", lambda e, qs=qs, pj=pj: e.activation(out=qs[:, 0:64], in_=pj[:, 0:64], func=AF.Copy),
                                 reads=[pjb], writes=[qsb])
                            P.op("act", lambda e, pj=pj, tile=tile: e.activation(out=wsc[:, tile, :], in_=pj[:, 64:72], func=AF.Copy,
                                                                                 scale=8.0 ** -0.5),
                                 reads=[pjb], writes=[WSC])
                            P.op("pool", lambda e, qb=qb, qs=qs: e.tensor_copy(out=qb[:, 0:64], in_=qs[:, 0:64]), reads=[qsb], writes=[qbb])
                            rope(qs[:, 0:64], qsb, qb[:, 0:64], qbb, 1, tile)
                            P.op("pool", lambda e, qb=qb: e.tensor_copy(out=qb[:, 64:128], in_=qb[:, 0:64]), reads=[qbb], writes=[qbb])
                            tp, tpb = tp_rot.next()
                            P.op("pe", lambda e, tp=tp, qb=qb: e.transpose(out=tp[:, 0, :], in_=qb[:, 0:128], identity=ident[:]),
                                 reads=[qbb, CONST], writes=[tpb])
                            P.op("act", lambda e, tp=tp, tile=tile: e.activation(out=KiT2[:, tile * 128:(tile + 1) * 128], in_=tp[:, 0, :],
                                                                                 func=AF.Copy),
                                 reads=[tpb], writes=[KIB])

                for i in range(NT):
                    sv = (i + 1) * 128
                    for sc in range((sv + 511) // 512):
                        ncol = min(512, sv - sc * 512)
                        for h in range(8):
                            base = (h % 2) * 64
                            stt, stb = st_rot.next()
                            tmp, tmpb = tmp_rot.next()
                            P.op("pe", lambda e, stt=stt, base=base, h=h, i=i, sc=sc, ncol=ncol: e.matmul(
                                stt[:, 0:ncol], lhsT=QiT[base:base + 64, h // 2, i * 128:(i + 1) * 128],
                                rhs=KiT2[base:base + 64, sc * 512:sc * 512 + ncol], start=True, stop=True),
                                reads=[QIB, KIB], writes=[stb])
                            P.op("act", lambda e, stt=stt, tmp=tmp, ncol=ncol: e.activation(out=tmp[:, 0:ncol], in_=stt[:, 0:ncol], func=AF.Relu),
                                 reads=[stb], writes=[tmpb])
                            a_sl = acc[:, sc * 512:sc * 512 + ncol]
                            if h == 0:
                                P.op("dve", lambda e, a_sl=a_sl, tmp=tmp, ncol=ncol, i=i, h=h: e.tensor_scalar(
                                    out=a_sl, in0=tmp[:, 0:ncol], scalar1=wsc[:, i, h:h + 1], scalar2=None, op0=ALU.mult),
                                    reads=[tmpb, WSC], writes=[ACC])
                            else:
                                P.op("dve", lambda e, a_sl=a_sl, tmp=tmp, ncol=ncol, i=i, h=h: e.scalar_tensor_tensor(
                                    out=a_sl, in0=tmp[:, 0:ncol], scalar=wsc[:, i, h:h + 1], in1=a_sl, op0=ALU.mult, op1=ALU.add),
                                    reads=[tmpb, WSC, ACC], writes=[ACC])
                    if i >= 2:
                        P.op("dve", lambda e, sv=sv: e.tensor_reduce(out=lo, in_=acc[:, 0:sv], axis=AX.X, op=ALU.min),
                             reads=[ACC], writes=[BIS])
                    dsl = acc[:, i * 128:(i + 1) * 128]
                    P.op("dve", lambda e, dsl=dsl: e.tensor_tensor(out=dsl, in0=dsl, in1=triq[:], op=ALU.add),
                         reads=[ACC, CONST], writes=[ACC])
                    if i >= 2:
                        P.op("dve", lambda e, sv=sv: e.tensor_reduce(out=hi, in_=acc[:, 0:sv], axis=AX.X, op=ALU.max),
                             reads=[ACC], writes=[BIS])
                        P.op("dve", lambda e: e.tensor_tensor(out=rng_, in0=hi, in1=lo, op=ALU.subtract), reads=[BIS], writes=[BIS])
                        P.op("dve", lambda e: e.tensor_scalar(out=steps2, in0=pow2[:, 0:NBIS + 1], scalar1=rng_, scalar2=None, op0=ALU.mult),
                             reads=[BIS, CONST], writes=[BIS])
                        P.op("dve", lambda e: e.tensor_scalar(out=steps, in0=steps2, scalar1=0.5, scalar2=None, op0=ALU.mult),
                             reads=[BIS], writes=[BIS])
                        P.op("dve", lambda e: e.tensor_tensor(out=mid, in0=lo, in1=steps[:, 0:1], op=ALU.add), reads=[BIS], writes=[MIDB])
                        h1 = max(1, int(0.42 * (i + 1) + 0.5)) * 128
                        n2 = sv - h1
                        thr_cnt = float(TOPK) - 0.5 * n2
                        for k in range(NBIS):
                            P.op("dve", lambda e, h1=h1: e.tensor_scalar(out=junk8[:, 0:h1], in0=acc[:, 0:h1], scalar1=mid, scalar2=None,
                                                                        op0=ALU.is_ge, op1=ALU.add, accum_out=cntv),
                                 reads=[ACC, MIDB], writes=[JK8, CNTB])
                            P.op("act", lambda e, h1=h1, sv=sv: e.activation(out=junk8[:, h1:sv], in_=acc[:, h1:sv], func=AF.Sign,
                                                                             bias=mid, scale=-1.0, accum_out=sga),
                                 reads=[ACC, MIDB], writes=[JK8A, SGAB])
                            P.op("dve", lambda e: e.scalar_tensor_tensor(out=cntv, in0=sga, scalar=-0.5, in1=cntv, op0=ALU.mult, op1=ALU.add),
                                 reads=[SGAB, CNTB], writes=[CNTB])
                            if k < NBIS - 1:
                                P.op("dve", lambda e, k=k, thr_cnt=thr_cnt: e.tensor_scalar(out=dd, in0=cntv, scalar1=thr_cnt,
                                                                                            scalar2=steps2[:, k + 1:k + 2],
                                                                                            op0=ALU.is_ge, op1=ALU.mult), reads=[CNTB, BIS], writes=[BIS])
                                P.op("dve", lambda e, k=k: e.scalar_tensor_tensor(out=mid, in0=dd, scalar=steps[:, k + 1:k + 2], in1=mid,
                                                                                  op0=ALU.subtract, op1=ALU.add), reads=[BIS, MIDB], writes=[MIDB])
                            else:
                                P.op("dve", lambda e, thr_cnt=thr_cnt: e.tensor_scalar(out=dd, in0=cntv, scalar1=thr_cnt, scalar2=-1.0,
                                                                                       op0=ALU.is_ge, op1=ALU.add), reads=[CNTB, BIS], writes=[BIS])
                                P.op("dve", lambda e, k=k: e.scalar_tensor_tensor(out=mid, in0=dd, scalar=steps[:, k:k + 1], in1=mid,
                                                                                  op0=ALU.mult, op1=ALU.add), reads=[BIS, MIDB], writes=[MIDB])
                        thr = mid
                    else:
                        thr = thrneg[:, 0:1]
                    P.op("dve", lambda e, sv=sv, thr=thr: e.tensor_scalar(out=mbf[:, 0:sv], in0=acc[:, 0:sv], scalar1=thr, scalar2=1.0,
                                                                          op0=ALU.is_ge, op1=ALU.subtract),
                         reads=[ACC, BIS, MIDB, CONST], writes=[MBB])
                    for j0 in range(0, i + 1, 8):
                        nj = min(8, i + 1 - j0)
                        for jj in range(nj):
                            j = j0 + jj
                            P.op("pe", lambda e, jj=jj, j=j: e.transpose(out=TPB[:, jj, :], in_=mbf[:, j * 128:(j + 1) * 128], identity=ident[:]),
                                 reads=[MBB, CONST], writes=TB if jj in (0, nj - 1) else [], inc=(jj == nj - 1))
                        P.op("act", lambda e, j0=j0, nj=nj: e.activation(out=mbT[:, j0:j0 + nj, :], in_=TPB[:, 0:nj, :], func=AF.Copy),
                             reads=TB, writes=[MBTB])
                    cch, u = i // 4, i % 4
                    njt = 4 * cch + 4
                    if njt > i + 1:
                        P.op("pool", lambda e, i=i, njt=njt: e.memset(mbT[:, i + 1:njt, :], -1.0), reads=[MBTB], writes=[MBTB])
                    P.dma(mk_d[cch, 0:njt, :, u * 128:(u + 1) * 128].rearrange("j s q -> s j q"), mbT[:, 0:njt, :], MBTB, reads=[MBTB])
                P.barrier()

            if "b" in phases:
                ml_rot = Rot([(arena[:, A_END + k * 512:A_END + (k + 1) * 512], P.buf("mload")) for k in range(8)])
                init_qk_tiles(False)
                for pp in range(4):
                    hq = 2 * pp * HD
                    prep_pair(l, 0, B_OFF + hq, B_OFF + 512 + hq, B_OFF + 1024 + hq, 1)
                    groups = []
                    for cch in range(8):
                        obs_slot = obs4_rot.next()
                        obank = [o4_rot.next(), o4_rot.next()]
                        nj = 4 * cch + 4
                        for j in range(nj):
                            ml, mlb = ml_rot.next()

                            def load_mask(ml=ml, mlb=mlb, cch=cch, j=j):
                                P.dma(ml, mk_d[cch, j, :, :], mlb, writes=[mlb])
                            for hh in range(2):
                                base = hh * 64
                                o4, o4b = obank[hh]

                                off = max(0, j - 4 * cch) * 128

                                def qk1(e, stt, hh=hh, j=j, cch=cch, off=off):
                                    return e.matmul(stt[:, off:512], lhsT=KT[hh][:, j * 128:(j + 1) * 128],
                                                    rhs=QT[0][:, cch * 512 + off:(cch + 1) * 512], start=True, stop=False)

                                def qk2(e, stt, ml=ml, off=off):
                                    return e.matmul(stt[:, off:512], lhsT=identB[:], rhs=ml[:, off:512], start=False, stop=True)
                                pv = []
                                for u in range(4):
                                    if j > 4 * cch + u:
                                        continue

                                    def pvf(e, ptt, o4=o4, j=j, u=u, hh=hh, cch=cch):
                                        return e.matmul(o4[:, u, 0:65], lhsT=ptt[:, u * 128:(u + 1) * 128], rhs=VA[:, j, hh, :],
                                                        start=(j == 0 and u == 0), stop=(j == 4 * cch + u), skip_group_check=True)
                                    pv.append((pvf, [VB], o4b))
                                after = []
                                if j == nj - 1:
                                    def fin(o4=o4, o4b=o4b, hh=hh, cch=cch, pp=pp, obs_slot=obs_slot):
                                        obs, obsb = obs_slot
                                        rc4, rc4b = rc4_rot.next()
                                        P.op("dve", lambda e: e.reciprocal(out=rc4, in_=o4[:, :, 64]), reads=[o4b], writes=[rc4b])
                                        P.op("dve", lambda e: e.tensor_tensor(out=obs[:, :, hh * 64:(hh + 1) * 64], in0=o4[:, :, 0:64],
                                                                              in1=rc4.unsqueeze(2).to_broadcast([128, 4, 64]), op=ALU.mult),
                                             reads=[o4b, rc4b], writes=[obsb])
                                        if hh == 1:
                                            P.dma(obc_d[4 * cch:4 * cch + 4, :, pp * 128:(pp + 1) * 128].rearrange("t p d -> p t d"), obs, obsb,
                                                  reads=[obsb], eng="pool")
                                    after.append(fin)
                                groups.append({"qk": [(qk1, [QKB]), (qk2, [CONST, mlb])], "n": 512, "n0": off, "pv": pv, "after": after,
                                               "before": [load_mask] if hh == 0 else []})
                    run_groups(groups)
                P.barrier()

            if "c" in phases:
                kmT = arena[:, A_END:A_END + 16]
                init_qk_tiles(True)
                KMB = P.buf("kmT")
                kmf = sb("kmf", [128, 16], F32) if l == 0 else kmf
                g16 = sb("g16", [128, 16], F32) if l == 0 else g16
                m8 = sb("m8", [128, 8], F32) if l == 0 else m8
                gA, gB, gC, gE = (arenaF[:, k * 512:(k + 1) * 512] for k in range(4))
                pmask = arenaF[:, 2048:2560]
                ownm = arenaF[:, 2560:3072]
                gm = sb("gm", [128, NT], F32) if l == 0 else gm
                sel32 = arena[:, A_END + 16:A_END + 16 + NT * 80].rearrange("p (t c) -> p t c", c=80)
                GSB, GMB, GEB, PMB = P.buf("gs"), P.buf("gm"), P.buf("ge"), P.buf("pm")
                pm4 = pmask.rearrange("p (o t b) -> p o t b", o=16, t=2)
                ow4 = ownm.rearrange("p (o t b) -> p o t b", o=16, t=2)
                P.op("pool", lambda e: e.memset(pmask, 0.0), writes=[PMB])
                P.op("pool", lambda e: e.affine_select(out=pm4, in_=pm4, pattern=[[1, 16], [0, 2], [-1, 16]], compare_op=ALU.is_ge,
                                                       fill=-1e30, base=-1, channel_multiplier=0), reads=[PMB], writes=[PMB])
                P.op("pool", lambda e: e.memset(ownm, 1.0), reads=[PMB], writes=[PMB])
                P.op("pool", lambda e: e.affine_select(out=ow4, in_=ow4, pattern=[[1, 16], [0, 2], [-1, 16]], compare_op=ALU.is_equal,
                                                       fill=0.0, base=0, channel_multiplier=0), reads=[PMB], writes=[PMB])
                S16 = P.buf("sel32")
                P.op("pool", lambda e: e.memset(sel32, 0.0), writes=[S16])
                for pp in range(4):
                    hq = 2 * pp * HD
                    prep_pair(l, 0, C_OFF + hq, C_OFF + 512 + hq, C_OFF + 1024 + hq, 2, moba=True)
                    P.op("dve", lambda e: e.tensor_reduce(out=kmf[0:64, :], in_=KT[0][0:64, :].rearrange("p (b k) -> p b k", k=256), axis=AX.X, op=ALU.add),
                         reads=[QKB], writes=[KMB])
                    P.op("dve", lambda e: e.tensor_reduce(out=kmf[64:128, :], in_=KT[1][64:128, :].rearrange("p (b k) -> p b k", k=256), axis=AX.X, op=ALU.add),
                         reads=[QKB], writes=[KMB])
                    P.op("dve", lambda e: e.tensor_scalar(out=kmT, in0=kmf[:], scalar1=1.0 / 256, scalar2=None, op0=ALU.mult),
                         reads=[KMB], writes=[KMB])
                    for hh in range(2):
                        base = hh * 64
                        c0 = 64 if hh == 0 else 0
                        nr = c0 + 16
                        gp, gpb = pj_rot.next()
                        for i in range(NT):
                            P.op("pe", lambda e, gp=gp, base=base, i=i, hh=hh: e.matmul(gp[:, i * 16:(i + 1) * 16],
                                                                                       lhsT=QT[hh][base:base + 64, i * 128:(i + 1) * 128],
                                                                                       rhs=kmT[base:base + 64, :], start=True, stop=True),
                                 reads=[QKB, KMB], writes=[gpb] if i in (0, NT - 1) else [], inc=(i == NT - 1))
                        P.op("dve", lambda e, gp=gp: e.tensor_tensor(out=gA, in0=gp[:, 0:512], in1=pmask, op=ALU.add),
                             reads=[gpb, PMB], writes=[GSB])
                        src = gA
                        for rnd, dst in enumerate((gB, gC, None)):
                            P.op("dve", lambda e, src=src: e.tensor_reduce(out=gm[:], in_=src.rearrange("p (t b) -> p t b", b=16), axis=AX.X, op=ALU.max),
                                 reads=[GSB], writes=[GMB])
                            if dst is None:
                                break
                            P.op("dve", lambda e, src=src: e.tensor_tensor(out=gE.rearrange("p (t b) -> p t b", b=16),
                                                                           in0=src.rearrange("p (t b) -> p t b", b=16),
                                                                           in1=gm[:].unsqueeze(2).to_broadcast([128, NT, 16]), op=ALU.is_ge),
                                 reads=[GSB, GMB], writes=[GEB])
                            P.op("dve", lambda e, src=src, dst=dst: e.scalar_tensor_tensor(out=dst, in0=gE, scalar=-1e30, in1=src, op0=ALU.mult, op1=ALU.add),
                                 reads=[GSB, GEB], writes=[GSB])
                            src = dst
                        P.op("dve", lambda e: e.tensor_tensor(out=gE.rearrange("p (t b) -> p t b", b=16), in0=gA.rearrange("p (t b) -> p t b", b=16),
                                                              in1=gm[:].unsqueeze(2).to_broadcast([128, NT, 16]), op=ALU.is_ge),
                             reads=[GSB, GMB], writes=[GEB])
                        P.op("dve", lambda e: e.tensor_tensor(out=gE, in0=gE, in1=ownm, op=ALU.max), reads=[GEB, PMB], writes=[GEB])
                        P.op("dve", lambda e, c0=c0: e.tensor_scalar(out=sel32[:, :, c0:c0 + 16], in0=gE.rearrange("p (t b) -> p t b", b=16),
                                                                     scalar1=-1.0, scalar2=None, op0=ALU.add),
                             reads=[GEB], writes=[S16])
                        for i0 in range(0, NT, 8):
                            for jj in range(8):
                                P.op("pe", lambda e, jj=jj, i0=i0, nr=nr: e.transpose(out=TPB[0:nr, jj, :], in_=sel32[:, i0 + jj, 0:nr], identity=ident[:]),
                                     reads=[S16, CONST], writes=TB if jj in (0, 7) else [], inc=(jj == 7))
                            P.op("act", lambda e, hh=hh, i0=i0, c0=c0: e.activation(
                                out=QT[hh][c0:c0 + 16, i0 * 128:(i0 + 8) * 128].rearrange("p (j q) -> p j q", q=128),
                                in_=TPB[c0:c0 + 16, :, :], func=AF.Copy), reads=TB, writes=[QKB])
                    groups = []
                    for cch in range(8):
                        obs_slot = obs4_rot.next()
                        obank = [o4_rot.next(), o4_rot.next()]
                        nj = 4 * cch + 4
                        for j in range(nj):
                            for hh in range(2):
                                base = hh * 64
                                o4, o4b = obank[hh]
                                qk = []

                                off = max(0, j - 4 * cch) * 128

                                def qk1(e, stt, hh=hh, j=j, cch=cch, off=off):
                                    return e.matmul(stt[:, off:512], lhsT=KT[hh][:, j * 128:(j + 1) * 128],
                                                    rhs=QT[hh][:, cch * 512 + off:(cch + 1) * 512], start=True, stop=(j < 4 * cch))
                                qk.append((qk1, [QKB]))
                                if j >= 4 * cch:
                                    def qk3(e, stt, uj=j - 4 * cch, off=off):
                                        return e.matmul(stt[:, off:512], lhsT=identB[:], rhs=dstrip[:, (3 - uj) * 128 + off:(3 - uj) * 128 + 512],
                                                        start=False, stop=True)
                                    qk.append((qk3, [CONST]))
                                pv = []
                                for u in range(4):
                                    if j > 4 * cch + u:
                                        continue

                                    def pvf(e, ptt, o4=o4, j=j, u=u, hh=hh, cch=cch):
                                        return e.matmul(o4[:, u, 0:65], lhsT=ptt[:, u * 128:(u + 1) * 128], rhs=VA[:, j, hh, :],
                                                        start=(j == 0 and u == 0), stop=(j == 4 * cch + u), skip_group_check=True)
                                    pv.append((pvf, [VB], o4b))
                                after = []
                                if j == nj - 1:
                                    def fin(o4=o4, o4b=o4b, hh=hh, cch=cch, pp=pp, obs_slot=obs_slot):
                                        obs, obsb = obs_slot
                                        rc4, rc4b = rc4_rot.next()
                                        P.op("dve", lambda e: e.reciprocal(out=rc4, in_=o4[:, :, 64]), reads=[o4b], writes=[rc4b])
                                        P.op("dve", lambda e: e.tensor_tensor(out=obs[:, :, hh * 64:(hh + 1) * 64], in0=o4[:, :, 0:64],
                                                                              in1=rc4.unsqueeze(2).to_broadcast([128, 4, 64]), op=ALU.mult),
                                             reads=[o4b, rc4b], writes=[obsb])
                                        if hh == 1:
                                            P.dma(obc_d[4 * cch:4 * cch + 4, :, 512 + pp * 128:512 + (pp + 1) * 128].rearrange("t p d -> p t d"),
                                                  obs, obsb, reads=[obsb])
                                    after.append(fin)
                                groups.append({"qk": qk, "n": 512, "n0": off, "pv": pv, "after": after})
                    run_groups(groups)
                P.barrier()

            if "z" in phases:
                zst_rot = Rot([(arena[:, k * 512:(k + 1) * 512], P.buf("zst")) for k in range(3)])
                for blk in range(9):
                    c0 = Z_OFF + blk * 512
                    tot = load_weights(l, [(c0, 512)])
                    cast_weights(tot)
                    for tile in range(NT):
                        pj, pjb = pj_rot.next()
                        project(0, tile, 0, 512, pj, pjb)
                        zs, zsb = zst_rot.next()
                        fn = AF.Silu if blk < 3 else AF.Sigmoid
                        P.op("act", lambda e, zs=zs, pj=pj, fn=fn: e.activation(out=zs, in_=pj[:, 0:512], func=fn), reads=[pjb], writes=[zsb])
                        if blk < 3:
                            P.dma(zs_d[tile, :, blk * 512:(blk + 1) * 512], zs, zsb, reads=[zsb])
                        else:
                            P.dma(gs_d[tile, :, (blk - 3) * 512:(blk - 2) * 512], zs, zsb, reads=[zsb])
                P.barrier()

            if "f" in phases:
                wbr = hTflat[:, 0:12288].rearrange("p (n c d) -> p n c d", n=3, c=4)
                wo = hTflat[:, 12288:20480].rearrange("p (c d) -> p c d", c=8)
                WF = P.buf("wfinal")
                for n in range(3):
                    for hf in range(2):
                        P.dma(wst[:, 0:4, :], wbr_d[l, n * 512:(n + 1) * 512, hf * 512:(hf + 1) * 512].rearrange("(c p) d -> p c d", p=128),
                              WST, writes=[WST])
                        P.op("pool", lambda e, n=n, hf=hf: e.tensor_copy(out=wbr[:, n, :, hf * 512:(hf + 1) * 512], in_=wst[:, 0:4, :]),
                             reads=[WST, HT], writes=[WF])
                for hf in range(2):
                    P.dma(wst[:, :, :], wout_d[l, :, hf * 512:(hf + 1) * 512].rearrange("(c p) d -> p c d", p=128), WST, writes=[WST])
                    P.op("pool", lambda e, hf=hf: e.tensor_copy(out=wo[:, :, hf * 512:(hf + 1) * 512], in_=wst[:, :, :]),
                         reads=[WST, HT], writes=[WF])
                xt_rot = Rot([(arenaF[:, i * 1024:(i + 1) * 1024], P.buf("fx")) for i in range(2)])
                oat = arenaF[:, 2048:2048 + 1560].rearrange("p (g h d) -> p g h d", g=3, h=8)
                OAT = P.buf("oat")
                msum = arenaF[:, 3608:3608 + 1024]
                MS = P.buf("msum")
                outt = arenaF[:, 4632:4632 + 1024]
                OUTT = P.buf("outt")
                den = sb("den", [128, 8], F32) if l == 0 else den
                DEN = P.buf("den")
                obt_rot = Rot([(arena[:, k * 1024:(k + 1) * 1024], P.buf("obt")) for k in range(2)])
                zt_rot = Rot([(arena[:, 2048 + k * 1536:2048 + (k + 1) * 1536], P.buf("zt")) for k in range(2)])
                gt_rot = Rot([(arena[:, 5120 + k * 3072:5120 + (k + 1) * 3072], P.buf("gt")) for k in range(2)])
                ub = arena[:, 11264:12800]
                UB = P.buf("ub")
                uT = arena[:, 12800:14336].rearrange("p (c t) -> p c t", c=12)
                UT = P.buf("uT")
                m16 = arena[:, 14336:15360]
                M16 = P.buf("m16")
                mT = arena[:, 15360:16384].rearrange("p (c t) -> p c t", c=8)
                MT = P.buf("mT")
                gy = arenaF[:, 5656:5656 + 512] if False else None
                uT_rot = Rot([(arena[:, 12800:14336].rearrange("p (c t) -> p c t", c=12), P.buf("uT")),
                                  (arena[:, 16384:17920].rearrange("p (c t) -> p c t", c=12), P.buf("uT"))])
                fst = {}

                def f_stage1(tt):
                        xt, xtb = xt_rot.next()
                        uT, UT = uT_rot.next()
                        obt, obtb = obt_rot.next()
                        zt, ztb = zt_rot.next()
                        gtile, gtb = gt_rot.next()
                        P.dma(xt, xsrc[tt * 128:(tt + 1) * 128, :], xtb, reads=[XS], writes=[xtb])
                        P.dma(oat.rearrange("p g h d -> p (g h d)"), oa_d[tt * 128:(tt + 1) * 128].rearrange("t g h d -> t (g h d)"), OAT, writes=[OAT])
                        P.dma(obt, obc_d[tt, :, :], obtb, writes=[obtb])
                        P.dma(zt, zs_d[tt, :, :], ztb, writes=[ztb])
                        P.dma(gtile, gs_d[tt, :, :], gtb, writes=[gtb])
                        P.op("dve", lambda e: e.tensor_tensor(out=oat[:, 0], in0=oat[:, 0], in1=oat[:, 1], op=ALU.add), reads=[OAT], writes=[OAT])
                        P.op("dve", lambda e: e.tensor_tensor(out=oat[:, 0], in0=oat[:, 0], in1=oat[:, 2], op=ALU.add), reads=[OAT], writes=[OAT])
                        P.op("dve", lambda e: e.reciprocal(out=den[:], in_=oat[:, 0, :, 64]), reads=[OAT], writes=[DEN])
                        P.op("dve", lambda e: e.tensor_tensor(out=oat[:, 1, :, 0:64], in0=oat[:, 0, :, 0:64],
                                                              in1=den[:].unsqueeze(2).to_broadcast([128, 8, 64]), op=ALU.mult),
                             reads=[OAT, DEN], writes=[OAT])
                        P.op("dve", lambda e, zt=zt: e.tensor_tensor(out=ub[:, 0:512].rearrange("p (h d) -> p h d", h=8), in0=oat[:, 1, :, 0:64],
                                                                     in1=zt[:, 0:512].rearrange("p (h d) -> p h d", h=8), op=ALU.mult),
                             reads=[OAT, ztb], writes=[UB])
                        P.op("pool", lambda e, zt=zt, obt=obt: e.tensor_tensor(out=ub[:, 512:1536], in0=obt, in1=zt[:, 512:1536], op=ALU.mult),
                             reads=[obtb, ztb], writes=[UB])
                        for half in range(2):
                            nchunk = 8 if half == 0 else 4
                            for cc in range(nchunk):
                                c = half * 8 + cc
                                P.op("pe", lambda e, c=c, cc=cc: e.transpose(out=TPB[:, cc, :], in_=ub[:, c * 128:(c + 1) * 128], identity=ident[:]),
                                     reads=[UB, CONST], writes=TB if cc in (0, nchunk - 1) else [], inc=(cc == nchunk - 1))
                            P.op("act", lambda e, half=half, nchunk=nchunk: e.activation(out=uT[:, half * 8:half * 8 + nchunk, :], in_=TPB[:, 0:nchunk, :],
                                                                                         func=AF.Copy), reads=TB, writes=[UT])
                        fst[tt] = (xt, xtb, gtile, gtb, uT, UT)

                def f_stage2(tt):
                        xt, xtb, gtile, gtb, uT, UT = fst.pop(tt)
                        for n in range(3):
                            for hf in range(2):
                                yp, ypb = st_rot.next()
                                for cc in range(4):
                                    P.op("pe", lambda e, yp=yp, n=n, cc=cc, hf=hf: e.matmul(yp[:, 0:512], lhsT=uT[:, n * 4 + cc, :],
                                                                                            rhs=wbr[:, n, cc, hf * 512:(hf + 1) * 512],
                                                                                            start=(cc == 0), stop=(cc == 3)),
                                         reads=[UT, WF], writes=[ypb] if cc in (0, 3) else [], inc=(cc == 3))
                                msl = msum[:, hf * 512:(hf + 1) * 512]
                                gsl = gtile[:, n * 1024 + hf * 512:n * 1024 + (hf + 1) * 512]
                                if n == 0:
                                    P.op("dve", lambda e, yp=yp, msl=msl, gsl=gsl: e.tensor_tensor(out=msl, in0=yp[:, 0:512], in1=gsl, op=ALU.mult),
                                         reads=[ypb, gtb], writes=[MS])
                                else:
                                    tmp, tmpb = qk_rot.next()
                                    P.op("dve", lambda e, yp=yp, tmp=tmp, gsl=gsl: e.tensor_tensor(out=tmp[:, 0:512], in0=yp[:, 0:512], in1=gsl, op=ALU.mult),
                                         reads=[ypb, gtb], writes=[tmpb])
                                    P.op("pool", lambda e, tmp=tmp, msl=msl: e.tensor_tensor(out=msl, in0=msl, in1=tmp[:, 0:512], op=ALU.add),
                                         reads=[tmpb, MS], writes=[MS])
                        P.op("pool", lambda e: e.tensor_copy(out=m16, in_=msum), reads=[MS], writes=[M16])
                        for cc in range(8):
                            P.op("pe", lambda e, cc=cc: e.transpose(out=TPB[:, cc, :], in_=m16[:, cc * 128:(cc + 1) * 128], identity=ident[:]),
                                 reads=[M16, CONST], writes=TB if cc in (0, 7) else [], inc=(cc == 7))
                        P.op("act", lambda e: e.activation(out=mT, in_=TPB[:, :, :], func=AF.Copy), reads=TB, writes=[MT])
                        for hf in range(2):
                            yp, ypb = st_rot.next()
                            for cc in range(8):
                                P.op("pe", lambda e, yp=yp, cc=cc, hf=hf: e.matmul(yp[:, 0:512], lhsT=mT[:, cc, :], rhs=wo[:, cc, hf * 512:(hf + 1) * 512],
                                                                                  start=(cc == 0), stop=(cc == 7)),
                                     reads=[MT, WF], writes=[ypb] if cc in (0, 7) else [], inc=(cc == 7))
                            P.op("dve", lambda e, yp=yp, hf=hf, xt=xt: e.tensor_tensor(out=outt[:, hf * 512:(hf + 1) * 512], in0=yp[:, 0:512],
                                                                                       in1=xt[:, hf * 512:(hf + 1) * 512], op=ALU.add),
                                 reads=[ypb, xtb], writes=[OUTT])
                        XD = XS if xdst is xsrc else P.buf("xd")
                        P.dma(xdst[tt * 128:(tt + 1) * 128, :], outt, OUTT, reads=[OUTT], writes=[])

                f_stage1(0)
                for tt in range(NT):
                    if tt + 1 < NT:
                        f_stage1(tt + 1)
                    f_stage2(tt)
                P.barrier()

        P.finish("sp")
        P.emit()
    print("bass program: %d instrs, %d sems" % (P.ninstr, len(P.sems)))
    return nc


def _perm_positions(pos_row):
    out = np.empty((128, 3 * NT), np.int32)
    for g in range(3):
        for tile in range(NT):
            out[:, g * NT + tile] = pos_row[tok_slice(g, tile)]
    return out


_CACHE = {}


def _get_prog(L):
    if L not in _CACHE:
        _CACHE[L] = build_program(L)
    return _CACHE[L]


FUSED = True


def kernel(x, positions, norm_g, w_in, qk_g, w_br, w_out):
    x = np.ascontiguousarray(np.asarray(x, np.float32))
    positions = np.asarray(positions, np.int32)
    norm_g = np.ascontiguousarray(np.asarray(norm_g, np.float32))
    w_in = np.ascontiguousarray(np.asarray(w_in, np.float32))
    qk_g2 = np.ascontiguousarray(np.asarray(qk_g, np.float32).reshape(2, 384))
    w_br2 = np.ascontiguousarray(np.asarray(w_br, np.float32).reshape(2, 1536, D))
    w_out = np.ascontiguousarray(np.asarray(w_out, np.float32))
    ncores = x.shape[0]
    posp = [_perm_positions(positions[b]) for b in range(ncores)]
    if FUSED:
        nc = _get_prog(2)
        in_maps = [{"x": x[b], "posp": posp[b], "norm_g": norm_g, "w_in": w_in, "qk_g": qk_g2, "w_br": w_br2, "w_out": w_out}
                   for b in range(ncores)]
        res = run_bass_kernel_spmd(nc, in_maps, core_ids=list(range(ncores)))
        return np.stack([np.asarray(r["out"], np.float32) for r in res.results], axis=0)
    nc = _get_prog(1)
    cur = [x[b] for b in range(ncores)]
    for l in range(2):
        in_maps = [{"x": cur[b], "posp": posp[b], "norm_g": norm_g[l:l + 1], "w_in": w_in[l:l + 1], "qk_g": qk_g2[l:l + 1],
                    "w_br": w_br2[l:l + 1], "w_out": w_out[l:l + 1]} for b in range(ncores)]
        res = run_bass_kernel_spmd(nc, in_maps, core_ids=list(range(ncores)))
        cur = [np.ascontiguousarray(np.asarray(r["out"], np.float32)) for r in res.results]
    return np.stack(cur, axis=0)
```
